# Optimizing a Trainium2 kernel written in Bass

```python
import math
import jax, jax.numpy as jnp
from jax import lax
import numpy as np

D_MODEL = 1024
BATCH = 16
SEQ = 2048
DEPTH = 2

HEAD_DIM = 64
SSM_WIDTH = D_MODEL // 4
SSM_GROUP_CH = 16
SSM_GROUPS = SSM_WIDTH // SSM_GROUP_CH
SSM_STATE = 64
DT_MIN = 1e-3
DT_MAX = 1e-1
SWA_HEADS = (3 * D_MODEL // 8) // HEAD_DIM
SWA_KV_HEADS = 2
SWA_WINDOW = 128
MOBA_HEADS = (3 * D_MODEL // 8) // HEAD_DIM
MOBA_KV_HEADS = 2
MOBA_BLOCK = 256
MOBA_TOPK = 3
MOBA_Q_CHUNK = 32
D_MIX = SSM_WIDTH + (SWA_HEADS + MOBA_HEADS) * HEAD_DIM
PROJ_SIZES = (
    SSM_WIDTH, SSM_WIDTH,
    SWA_HEADS * HEAD_DIM, SWA_KV_HEADS * HEAD_DIM, SWA_KV_HEADS * HEAD_DIM, SWA_HEADS * HEAD_DIM,
    MOBA_HEADS * HEAD_DIM, MOBA_KV_HEADS * HEAD_DIM, MOBA_KV_HEADS * HEAD_DIM, MOBA_HEADS * HEAD_DIM,
)
D_PROJ = sum(PROJ_SIZES)
RMS_EPS = 1e-6

kernel_name = 'hybrid_s5_swa_moba_block'


def rms_norm(x, g):
    xf = x.astype(jnp.float32)
    y = xf * lax.rsqrt(jnp.mean(xf * xf, axis=-1, keepdims=True) + RMS_EPS)
    return (y * g.astype(jnp.float32)).astype(x.dtype)


def s5_mixer(u, lam_re, lam_im, log_dt, b_re, b_im, c_re, c_im, d, glu_w, glu_b):
    bsz, seqlen, width = u.shape
    f32 = jnp.float32
    uf = u.astype(f32).reshape(bsz, seqlen, SSM_GROUPS, SSM_GROUP_CH)
    dt = jnp.exp(log_dt.astype(f32))[:, None]
    lr = lam_re.astype(f32)
    li = lam_im.astype(f32)
    mag = jnp.exp(lr * dt)
    abar_re = mag * jnp.cos(li * dt)
    abar_im = mag * jnp.sin(li * dt)
    den = lr * lr + li * li
    nr = abar_re - 1.0
    ni = abar_im
    cr = (nr * lr + ni * li) / den
    ci = (ni * lr - nr * li) / den
    br = b_re.astype(f32)
    bi = b_im.astype(f32)
    bbar_re = cr[..., None] * br - ci[..., None] * bi
    bbar_im = cr[..., None] * bi + ci[..., None] * br
    bu_re = jnp.einsum('blgh,gph->blgp', uf, bbar_re)
    bu_im = jnp.einsum('blgh,gph->blgp', uf, bbar_im)
    a_re = jnp.broadcast_to(abar_re, bu_re.shape)
    a_im = jnp.broadcast_to(abar_im, bu_im.shape)

    def combine(e1, e2):
        a1r, a1i, b1r, b1i = e1
        a2r, a2i, b2r, b2i = e2
        return (a2r * a1r - a2i * a1i,
                a2r * a1i + a2i * a1r,
                a2r * b1r - a2i * b1i + b2r,
                a2r * b1i + a2i * b1r + b2i)

    _, _, xs_re, xs_im = lax.associative_scan(combine, (a_re, a_im, bu_re, bu_im), axis=1)
    y = (jnp.einsum('blgp,ghp->blgh', xs_re, c_re.astype(f32))
         - jnp.einsum('blgp,ghp->blgh', xs_im, c_im.astype(f32)))
    y = y.reshape(bsz, seqlen, width) + d.astype(f32) * u.astype(f32)
    y = jax.nn.gelu(y)
    y = y * jax.nn.sigmoid(jnp.einsum('ble,ef->blf', y, glu_w.astype(f32)) + glu_b.astype(f32))
    return y.astype(u.dtype)


def swa_attention(q, k, v, sink):
    bsz, seqlen, n_q, hd = q.shape
    n_kv = k.shape[2]
    grp = n_q // n_kv
    w = SWA_WINDOW
    nblk = seqlen // w
    qb = q.reshape(bsz, nblk, w, n_kv, grp, hd)

    def band(a):
        prev = jnp.pad(a, ((0, 0), (w, 0), (0, 0), (0, 0)))[:, :seqlen]
        return jnp.concatenate([prev.reshape(bsz, nblk, w, n_kv, hd),
                                a.reshape(bsz, nblk, w, n_kv, hd)], axis=2)

    kb = band(k)
    vb = band(v)
    s = jnp.einsum('bnqhgd,bnshd->bnhgqs', qb, kb).astype(jnp.float32) * (hd ** -0.5)
    qi = jnp.arange(w)[:, None]
    si = jnp.arange(2 * w)[None, :]
    rel = qi + w - si
    in_win = (rel >= 0) & (rel < w)
    blk = jnp.arange(nblk)[:, None, None]
    mask = in_win[None] & ((blk > 0) | (si[None] >= w))
    s = jnp.where(mask[None, :, None, None], s, -jnp.inf)
    sink_l = jnp.broadcast_to(sink.astype(jnp.float32).reshape(1, 1, n_kv, grp, 1, 1),
                              s.shape[:-1] + (1,))
    p = jax.nn.softmax(jnp.concatenate([s, sink_l], axis=-1), axis=-1)[..., :-1]
    o = jnp.einsum('bnhgqs,bnshd->bnqhgd', p.astype(vb.dtype), vb)
    return o.reshape(bsz, seqlen, n_q * hd)


def moba_attention(q, k, v):
    bsz, seqlen, n_q, hd = q.shape
    n_kv = k.shape[2]
    grp = n_q // n_kv
    f32 = jnp.float32
    nb = -(-seqlen // MOBA_BLOCK)
    padded = nb * MOBA_BLOCK
    pad = ((0, 0), (0, padded - seqlen), (0, 0), (0, 0))
    kp = jnp.pad(k, pad).reshape(bsz, nb, MOBA_BLOCK, n_kv, hd)
    vp = jnp.pad(v, pad).reshape(bsz, nb, MOBA_BLOCK, n_kv, hd)
    k_mean = jnp.mean(kp.astype(f32), axis=2)
    kb = kp.transpose(0, 3, 1, 2, 4)
    vb = vp.transpose(0, 3, 1, 2, 4)
    qg = q.reshape(bsz, seqlen, n_kv, grp, hd)
    pos = jnp.arange(seqlen, dtype=jnp.int32)
    q_blk = pos // MOBA_BLOCK
    gate = jnp.einsum('blhgd,bnhd->blhn', qg.astype(f32), k_mean)
    fully_past = jnp.arange(nb, dtype=jnp.int32)[None, :] < q_blk[:, None]
    gate = jnp.where(fully_past[None, :, None, :], gate, -jnp.inf)
    n_sel = min(MOBA_TOPK, nb)
    _, sel = lax.top_k(gate, n_sel)
    sel = sel.astype(jnp.int32)
    sel_ok = sel < q_blk[None, :, None, None]
    own = jnp.broadcast_to(q_blk[None, :, None, None], (bsz, seqlen, n_kv, 1))
    idx = jnp.concatenate([sel, own], axis=-1)
    slot_ok = jnp.concatenate([sel_ok, jnp.ones(own.shape, dtype=bool)], axis=-1)
    n_chunks = seqlen // MOBA_Q_CHUNK

    def to_chunks(a):
        return jnp.moveaxis(a.reshape((bsz, n_chunks, MOBA_Q_CHUNK) + a.shape[2:]), 1, 0)

    b_ix = jnp.arange(bsz)[:, None, None, None]
    h_ix = jnp.arange(n_kv)[None, None, :, None]
    blk_off = jnp.arange(MOBA_BLOCK, dtype=jnp.int32)
    scale = hd ** -0.5

    def attend_chunk(args):
        qc, ic, oc, pc = args
        kg = kb[b_ix, h_ix, ic]
        vg = vb[b_ix, h_ix, ic]
        s = jnp.einsum('bchgd,bchnkd->bchgnk', qc, kg).astype(f32) * scale
        key_pos = ic[..., None] * MOBA_BLOCK + blk_off
        ok = oc[..., None] & (key_pos <= pc[None, :, None, None, None])
        s = jnp.where(ok[:, :, :, None], s, -jnp.inf)
        shp = s.shape
        p = jax.nn.softmax(s.reshape(shp[:4] + (-1,)), axis=-1).reshape(shp)
        return jnp.einsum('bchgnk,bchnkd->bchgd', p.astype(vg.dtype), vg)

    out = lax.map(attend_chunk, (to_chunks(qg), to_chunks(idx), to_chunks(slot_ok),
                                 pos.reshape(n_chunks, MOBA_Q_CHUNK)))
    return jnp.moveaxis(out, 0, 1).reshape(bsz, seqlen, n_q * hd)


def hybrid_layer(x, norm_g, w_in, lam_re, lam_im, log_dt, b_re, b_im, c_re, c_im, d,
                 glu_w, glu_b, swa_q_norm, swa_k_norm, swa_sink, moba_q_norm, moba_k_norm, w_out):
    bsz, seqlen, _ = x.shape
    h = rms_norm(x, norm_g)
    proj = jnp.einsum('bld,de->ble', h, w_in)
    parts = []
    off = 0
    for n in PROJ_SIZES:
        parts.append(proj[..., off:off + n])
        off += n
    s_u, s_g, a_q, a_k, a_v, a_g, m_q, m_k, m_v, m_g = parts

    def heads(t, n):
        return t.reshape(bsz, seqlen, n, HEAD_DIM)

    y_ssm = s5_mixer(s_u, lam_re, lam_im, log_dt, b_re, b_im, c_re, c_im, d, glu_w, glu_b) * jax.nn.silu(s_g)
    aq = rms_norm(heads(a_q, SWA_HEADS), swa_q_norm)
    ak = rms_norm(heads(a_k, SWA_KV_HEADS), swa_k_norm)
    y_swa = swa_attention(aq, ak, heads(a_v, SWA_KV_HEADS), swa_sink) * jax.nn.silu(a_g)
    mq = rms_norm(heads(m_q, MOBA_HEADS), moba_q_norm)
    mk = rms_norm(heads(m_k, MOBA_KV_HEADS), moba_k_norm)
    y_moba = moba_attention(mq, mk, heads(m_v, MOBA_KV_HEADS)) * jax.nn.silu(m_g)
    y = jnp.concatenate([y_ssm, y_swa.astype(y_ssm.dtype), y_moba.astype(y_ssm.dtype)], axis=-1)
    return x + jnp.einsum('ble,ed->bld', y, w_out).astype(x.dtype)


def setup_inputs(seed: int = 0) -> dict:
    key = jax.random.key(seed)
    ks = jax.random.split(key, 19)
    f32 = jnp.float32
    nrm = jax.random.normal
    G, P, H = SSM_GROUPS, SSM_STATE, SSM_GROUP_CH
    n_idx = jnp.arange(P, dtype=f32)
    return {
        'x': nrm(ks[0], (BATCH, SEQ, D_MODEL), f32),
        'norm_g': 1.0 + 0.02 * nrm(ks[1], (DEPTH, D_MODEL), f32),
        'w_in': nrm(ks[2], (DEPTH, D_MODEL, D_PROJ), f32) * D_MODEL ** -0.5,
        'ssm_lam_re': -0.5 + 0.01 * nrm(ks[3], (DEPTH, G, P), f32),
        'ssm_lam_im': math.pi * n_idx + 0.01 * nrm(ks[4], (DEPTH, G, P), f32),
        'ssm_log_dt': jax.random.uniform(ks[5], (DEPTH, G), f32, math.log(DT_MIN), math.log(DT_MAX)),
        'ssm_b_re': nrm(ks[6], (DEPTH, G, P, H), f32) * (2 * H) ** -0.5,
        'ssm_b_im': nrm(ks[7], (DEPTH, G, P, H), f32) * (2 * H) ** -0.5,
        'ssm_c_re': nrm(ks[8], (DEPTH, G, H, P), f32) * P ** -0.5,
        'ssm_c_im': nrm(ks[9], (DEPTH, G, H, P), f32) * P ** -0.5,
        'ssm_d': nrm(ks[10], (DEPTH, SSM_WIDTH), f32),
        'ssm_glu_w': nrm(ks[11], (DEPTH, SSM_WIDTH, SSM_WIDTH), f32) * SSM_WIDTH ** -0.5,
        'ssm_glu_b': 0.02 * nrm(ks[12], (DEPTH, SSM_WIDTH), f32),
        'swa_q_norm': 1.0 + 0.02 * nrm(ks[13], (DEPTH, HEAD_DIM), f32),
        'swa_k_norm': 1.0 + 0.02 * nrm(ks[14], (DEPTH, HEAD_DIM), f32),
        'swa_sink': 0.5 * nrm(ks[15], (DEPTH, SWA_HEADS), f32),
        'moba_q_norm': 1.0 + 0.02 * nrm(ks[16], (DEPTH, HEAD_DIM), f32),
        'moba_k_norm': 1.0 + 0.02 * nrm(ks[17], (DEPTH, HEAD_DIM), f32),
        'w_out': nrm(ks[18], (DEPTH, D_MIX, D_MODEL), f32) * D_MIX ** -0.5,
    }


def reference(x, norm_g, w_in, ssm_lam_re, ssm_lam_im, ssm_log_dt, ssm_b_re, ssm_b_im,
              ssm_c_re, ssm_c_im, ssm_d, ssm_glu_w, ssm_glu_b, swa_q_norm, swa_k_norm,
              swa_sink, moba_q_norm, moba_k_norm, w_out):
    for l in range(DEPTH):
        x = hybrid_layer(x, norm_g[l], w_in[l], ssm_lam_re[l], ssm_lam_im[l], ssm_log_dt[l],
                         ssm_b_re[l], ssm_b_im[l], ssm_c_re[l], ssm_c_im[l], ssm_d[l],
                         ssm_glu_w[l], ssm_glu_b[l], swa_q_norm[l], swa_k_norm[l], swa_sink[l],
                         moba_q_norm[l], moba_k_norm[l], w_out[l])
    return x
```

```python
from contextlib import ExitStack
import numpy as np
import ml_dtypes
import concourse.bass as bass
import concourse.mybir as mybir
from concourse.bass_utils import run_bass_kernel_spmd

F32 = mybir.dt.float32
BF16 = mybir.dt.bfloat16
ALU = mybir.AluOpType
AF = mybir.ActivationFunctionType
AX = mybir.AxisListType

L = 2048
D = 1024
NT = 16
EPS = 1e-6
NEGBIG = -30000.0
MAGIC = 12582912.0
TWO_PI = float(2 * np.pi)


class Buf:
    __slots__ = ("name", "w", "r", "dsem", "dcnt")

    def __init__(self, name):
        self.name = name
        self.w = []
        self.r = []
        self.dsem = None
        self.dcnt = 0


class Prog:
    ENGS = ("pe", "dve", "act", "pool", "sp")

    def __init__(self, nc, stack):
        self.nc = nc
        self.stack = stack
        self.E = {}
        for e in self.ENGS:
            self.E[e] = dict(ops=[], sem=stack.enter_context(nc.semaphore("s_" + e)), cnt=0, waited={})
        self.nbuf = 0
        self.out_tickets = []
        self.dbufs = []

    def buf(self, name=None):
        self.nbuf += 1
        return Buf(name or f"b{self.nbuf}")

    def bufs(self, n, name="b"):
        return [self.buf(f"{name}{i}") for i in range(n)]

    def sb(self, name, shape, dtype):
        return self.stack.enter_context(self.nc.sbuf_tensor(name, list(shape), dtype))

    def ps(self, name, shape, dtype):
        return self.stack.enter_context(self.nc.psum_tensor(name, list(shape), dtype))

    def _waits(self, eng, reads, writes, skip_sem=None):
        E = self.E[eng]
        deps = []
        for b in reads:
            deps += b.w
        for b in writes:
            deps += b.w
            deps += b.r
        waits = []
        for (sem, val, src) in deps:
            if eng == "pe" and src == "pe":
                continue
            if skip_sem is not None and sem is skip_sem:
                continue
            if E["waited"].get(sem.num, 0) < val:
                E["waited"][sem.num] = val
                waits.append((sem, val))
        return waits

    def op(self, eng, fn, reads=(), writes=()):
        E = self.E[eng]
        waits = self._waits(eng, reads, writes)
        E["cnt"] += 1
        t = (E["sem"], E["cnt"], eng)
        E["ops"].append((waits, fn, (E["sem"], 1)))
        for b in reads:
            b.r.append(t)
        for b in writes:
            b.w = [t]
            b.r = []
        return t

    def dma(self, eng, out, in_, reads=(), writes=(), is_output=False, **kw):
        E = self.E[eng]
        key = writes[0] if writes else reads[0]
        if key.dsem is None:
            key.dsem = self.stack.enter_context(self.nc.semaphore(f"d{len(self.dbufs)}_" + key.name))
            self.dbufs.append(key)
        waits = self._waits(eng, reads, writes, skip_sem=key.dsem)
        key.dcnt += 1
        t = (key.dsem, 16 * key.dcnt, "dma")
        E["ops"].append((waits, (lambda e: e.dma_start(out=out, in_=in_, **kw)), (key.dsem, 16)))
        for b in reads:
            b.r.append(t)
        for b in writes:
            b.w = [t]
            b.r = []
        if is_output:
            self.out_tickets.append(t)
        return t

    def barrier(self):
        ticks = [(self.E[e]["sem"], self.E[e]["cnt"]) for e in self.ENGS if self.E[e]["cnt"] > 0]
        ticks += [(b.dsem, 16 * b.dcnt) for b in self.dbufs]
        for e in self.ENGS:
            E = self.E[e]
            waits = []
            for (sem, val) in ticks:
                if sem is E["sem"]:
                    continue
                if E["waited"].get(sem.num, 0) < val:
                    E["waited"][sem.num] = val
                    waits.append((sem, val))
            if waits:
                E["ops"].append((waits, None, None))

    def finish(self):
        self.barrier()

    def emit(self):
        nc = self.nc
        P = self

        def run(name, e):
            for (waits, fn, inc) in P.E[name]["ops"]:
                for (sem, val) in waits:
                    e.wait_ge(sem, val)
                if fn is None:
                    continue
                ins = fn(e)
                ins.then_inc(inc[0], inc[1])

        with nc.Block() as block:
            @block.tensor
            def _(e):
                run("pe", e)

            @block.vector
            def _(e):
                run("dve", e)

            @block.scalar
            def _(e):
                run("act", e)

            @block.gpsimd
            def _(e):
                run("pool", e)

            @block.sync
            def _(e):
                run("sp", e)


def _host_consts():
    c = {}
    c["ident"] = np.eye(128, dtype=np.float32)
    bo = np.zeros((128, 128), np.float32)
    bo[:64, :64] = 1.0
    bo[64:, 64:] = 1.0
    c["bones"] = bo
    s_idx = np.arange(128) // 16
    c["mask8"] = (s_idx[:, None] <= s_idx[None, :]).astype(np.float32)
    c["cidx"] = np.tile(np.arange(256, dtype=np.float32)[None, :], (128, 1))
    ev = np.concatenate([-np.arange(8), 7 - np.arange(8), np.arange(9)]).astype(np.float32)
    c["evals"] = np.tile(ev[None, :], (128, 1))
    sg = np.ones((128, 2), np.float32)
    sg[:64, 0] = -1.0
    sg[64:, 1] = -1.0
    c["sgn"] = sg
    k = np.arange(128)[:, None]
    q = np.arange(128)[None, :]
    diag = np.where(k <= q, 0.0, NEGBIG).astype(np.float32)
    prev = np.where(k > q, 0.0, NEGBIG).astype(np.float32)
    c["amask"] = np.stack([np.tile(diag, (1, 3)), np.tile(prev, (1, 3))], axis=1)
    ka = np.zeros((8, 8, 128), np.float32)
    for n in range(8):
        ka[n, n, :] = NEGBIG
    c["kaug"] = ka
    bm = np.zeros((128, 16, 2, 8), np.float32)
    for i in range(16):
        qb = i // 2
        bm[:, i, :, qb:] = -1e30
    c["bmask"] = bm
    return c


def _host_params(inp):
    f = lambda a: np.ascontiguousarray(np.asarray(a, dtype=np.float32))
    o = {}
    o["normg"] = f(inp["norm_g"].reshape(2, 8, 128).transpose(2, 0, 1))
    lr = inp["ssm_lam_re"].transpose(2, 0, 1)
    li = inp["ssm_lam_im"].transpose(2, 0, 1)
    ld = np.broadcast_to(inp["ssm_log_dt"][None, :, :], (64, 2, 16))
    A = np.stack([lr, li, ld], axis=2)
    o["ssmA"] = f(np.concatenate([A, A], axis=0))
    br = inp["ssm_b_re"].transpose(2, 0, 1, 3).reshape(64, 2, 256)
    bi = inp["ssm_b_im"].transpose(2, 0, 1, 3).reshape(64, 2, 256)
    o["ssmB"] = f(np.stack([np.concatenate([br, bi], 0), np.concatenate([bi, br], 0)], axis=2))
    cr = inp["ssm_c_re"].transpose(3, 0, 1, 2).reshape(64, 2, 256)
    ci = inp["ssm_c_im"].transpose(3, 0, 1, 2).reshape(64, 2, 256)
    o["ssmC"] = f(np.stack([np.concatenate([cr, ci], 0), np.concatenate([ci, cr], 0)], axis=2))
    d = inp["ssm_d"].reshape(2, 16, 16)
    o["dcol"] = f(np.tile(d.transpose(2, 0, 1)[None], (8, 1, 1, 1)).reshape(128, 2, 16))
    o["glub"] = f(inp["ssm_glu_b"].reshape(2, 2, 128).transpose(2, 0, 1))
    hn = np.stack([inp["swa_q_norm"], inp["swa_k_norm"], inp["moba_q_norm"], inp["moba_k_norm"]], axis=1)
    hn = hn.transpose(2, 0, 1)
    o["hn"] = f(np.concatenate([hn, hn], 0))
    o["sink"] = f(np.tile(inp["swa_sink"][None], (128, 1, 1)))
    return o


PARAM_SHAPES = {
    "normg": [128, 2, 8], "ssmA": [128, 2, 3, 16], "ssmB": [128, 2, 2, 256], "ssmC": [128, 2, 2, 256],
    "dcol": [128, 2, 16], "glub": [128, 2, 2], "hn": [128, 2, 4], "sink": [128, 2, 6],
}
CONST_SHAPES = {
    "ident": [128, 128], "bones": [128, 128], "mask8": [128, 128], "cidx": [128, 256], "evals": [128, 25],
    "sgn": [128, 2], "amask": [128, 2, 384], "kaug": [8, 8, 128], "bmask": [128, 16, 2, 8],
}


def _prod(s):
    r = 1
    for v in s:
        r *= v
    return r


def build(nseq, taps=None, stop=None):
    nc = bass.Bass("TRN2", target_bir_lowering=False)
    dram = {}
    x_d = nc.dram_tensor("x", [nseq, L, D], F32, kind="ExternalInput").ap()
    out_d = nc.dram_tensor("out", [nseq, L, D], F32, kind="ExternalOutput").ap()
    w_in_d = nc.dram_tensor("w_in", [2, D, 2560], F32, kind="ExternalInput").ap()
    w_out_d = nc.dram_tensor("w_out", [2, D, D], F32, kind="ExternalInput").ap()
    gluw_d = nc.dram_tensor("gluw", [2, 256, 256], F32, kind="ExternalInput").ap()
    for k, shp in list(PARAM_SHAPES.items()) + list(CONST_SHAPES.items()):
        dram[k] = nc.dram_tensor(k, shp, F32, kind="ExternalInput").ap()
    TB_d = nc.dram_tensor("TBs", [2, 128, 16 * 5 * 128], BF16, kind="Internal").ap()
    ROT_d = nc.dram_tensor("ROTs", [2, 128, 16 * 2 * 256], F32, kind="Internal").ap()
    tap_out = {}

    with ExitStack() as st:
        P = Prog(nc, st)
        ARENA = 53200
        arena = P.sb("arena", [128, ARENA], F32)

        def carve(off, shape, dtype, parts=128):
            n = _prod(shape[1:])
            esz = 4 if dtype == F32 else 2
            assert off % 4 == 0
            nw = (n * esz + 3) // 4
            assert off // 4 + nw <= ARENA, (off, shape)
            a = arena[0:parts, off // 4: off // 4 + nw]
            if dtype != F32:
                a = a.bitcast(dtype)[:, 0:n]
            if len(shape) == 3:
                a = a.rearrange("p (a b) -> p a b", a=shape[1])
            elif len(shape) == 4:
                a = a.rearrange("p (a b c) -> p a b c", a=shape[1], b=shape[2])
            elif len(shape) == 5:
                a = a.rearrange("p (a b c d) -> p a b c d", a=shape[1], b=shape[2], c=shape[3])
            return a

        def tap(name, ap, b, dtype=None):
            if taps is None or name not in taps or name in tap_out:
                return
            dt = dtype or ap.dtype
            d_ = nc.dram_tensor("tap_" + name, list(ap.shape), dt, kind="ExternalOutput").ap()
            tb = P.buf("tap_" + name)
            P.dma("sp", d_, ap, reads=list(b) if isinstance(b, (list, tuple)) else [b], writes=[tb], is_output=True)
            tap_out[name] = "tap_" + name

        X = carve(0, [128, 16, 1024], F32)
        XB = P.bufs(16, "X")
        hT = carve(65536, [128, 8, 2048], BF16)
        hTB = P.bufs(4, "hT")
        yT = carve(98304, [128, 8, 2048], BF16)
        yTB = {(c, n): P.buf(f"yT{c}_{n}") for c in range(8) for n in range(4)}
        Wb = [carve(131072, [128, 8, 512], BF16), carve(139264, [128, 8, 512], BF16)]
        WbB = P.bufs(2, "Wb")
        po = [147456]

        def pers(shape, dtype, parts=128):
            n = _prod(shape[1:]) * (4 if dtype == F32 else 2)
            n = (n + 3) // 4 * 4
            a = carve(po[0], shape, dtype, parts)
            po[0] += n
            return a

        identf = pers([128, 128], F32)
        identb = pers([128, 128], BF16)
        bones = pers([128, 128], BF16)
        amask = pers([128, 2, 384], BF16)
        kaug = pers([128, 8, 128], BF16)
        bmask = pers([128, 16, 2, 8], F32)
        normg = pers([128, 2, 8, 1], F32)
        glub = pers([128, 2, 2], F32)
        hn = pers([128, 2, 4], F32)
        es = pers([128, 2, 6], F32)
        rho = pers([128, 2, 16], F32)
        sgn = pers([128, 2], F32)
        gluw = pers([128, 2, 2, 256], BF16)
        ssq = pers([128, 16], F32)
        srt = pers([128, 16], F32)
        rstd = pers([128, 16], F32)
        hb = [pers([128, 1024], BF16), pers([128, 1024], BF16)]
        hbB = P.bufs(2, "hb")
        ones_b = pers([128, 128], BF16)
        cB = P.buf("consts")
        ssqB, srtB, rstdB = P.buf("ssq"), P.buf("srt"), P.buf("rstd")
        T0 = po[0]
        assert T0 <= 160768, T0
        T0 = 160768

        PSbig = [P.ps(f"psb{i}", [128, 1024], F32)[:] for i in range(4)]
        PS = []
        for i_ in range(4):
            PS.append(PSbig[i_][:, 0:512])
            PS.append(PSbig[i_][:, 512:1024])
        PSB = P.bufs(8, "ps")
        REG = {}

        def pbuf(name):
            if name not in REG:
                REG[name] = P.buf(name)
            return REG[name]

        def pbufs(n, name):
            return [pbuf(f"{name}{i}") for i in range(n)]

        def group(prefix):
            return [b for k_, b in REG.items() if k_.startswith(prefix)]

        fscr = pers([128, 4], F32)
        assert po[0] <= 160768 - 16, po[0]

        def fence(dead, new):
            allb = []
            for b in list(dead) + list(new):
                if b not in allb:
                    allb.append(b)
            P.op("pool", lambda e: e.memset(fscr, 0.0), writes=allb)

        def ld(dst, src, eng="sp"):
            P.dma(eng, dst, src, writes=[cB])

        ld(identf, dram["ident"])
        ld(identb, dram["ident"], "pool")
        ld(bones, dram["bones"], "pool")
        ld(amask, dram["amask"], "pool")
        P.op("pool", lambda e: e.memset(kaug, 0.0), writes=[cB])
        ld(kaug[0:8], dram["kaug"], "pool")
        ld(bmask, dram["bmask"])
        ld(normg[:, :, :, 0], dram["normg"])
        ld(glub, dram["glub"])
        ld(hn, dram["hn"])
        ld(es, dram["sink"])
        ld(sgn, dram["sgn"])
        for l in range(2):
            ld(gluw[:, l], gluw_d[l].rearrange("(e p) f -> p e f", p=128), "pool")
        P.op("act", lambda e: e.activation(out=es, in_=es, func=AF.Exp), reads=[cB], writes=[cB])
        P.op("pool", lambda e: e.memset(ones_b, 1.0), writes=[cB])

        def prologue(l):
            o = [0]

            def tmp(shape, dtype=F32):
                n = _prod(shape[1:]) * (4 if dtype == F32 else 2)
                a = carve(o[0], shape, dtype)
                o[0] += (n + 3) // 4 * 4
                return a

            A = tmp([128, 3, 16])
            Bp = tmp([128, 2, 16, 16])
            Cp = tmp([128, 2, 16, 16])
            dcol = tmp([128, 16])
            mask8 = tmp([128, 128])
            cidx = tmp([128, 256])
            ev = tmp([128, 25])
            b0 = P.buf("pro_in")
            P.dma("sp", A, dram["ssmA"][:, l], writes=[b0])
            P.dma("sp", Bp, dram["ssmB"][:, l].rearrange("p a (g h) -> p a g h", g=16), writes=[b0])
            P.dma("sp", Cp, dram["ssmC"][:, l].rearrange("p a (g h) -> p a g h", g=16), writes=[b0])
            P.dma("sp", dcol, dram["dcol"][:, l], writes=[b0])
            P.dma("sp", mask8, dram["mask8"], writes=[b0])
            P.dma("sp", cidx, dram["cidx"], writes=[b0])
            P.dma("sp", ev, dram["evals"], writes=[b0])
            pb = P.buf("pro")

            def V(fn, eng="dve"):
                P.op(eng, fn, reads=[b0, cB], writes=[pb])

            dt = tmp([128, 16])
            a_ = tmp([128, 16, 1])
            b_ = tmp([128, 16, 1])
            V(lambda e: e.activation(out=dt, in_=A[:, 2], func=AF.Exp), "act")
            V(lambda e: e.tensor_tensor(out=a_[:, :, 0], in0=A[:, 0], in1=dt, op=ALU.mult))
            V(lambda e: e.tensor_tensor(out=b_[:, :, 0], in0=A[:, 1], in1=dt, op=ALU.mult))
            NE = 25
            EA = tmp([128, 16, NE])
            ANG = tmp([128, 16, NE])
            KT = tmp([128, 16, NE])
            bc_e = lambda t: t.to_broadcast([128, 16, NE])
            evb = ev.unsqueeze(1).to_broadcast([128, 16, NE])
            V(lambda e: e.tensor_tensor(out=EA, in0=bc_e(a_), in1=evb, op=ALU.mult))
            V(lambda e: e.tensor_tensor(out=ANG, in0=bc_e(b_), in1=evb, op=ALU.mult))
            mag = tmp([128, 16, NE])
            V(lambda e: e.activation(out=mag, in_=EA, func=AF.Exp), "act")

            def sincos(ang, kt, sin_out, cos_out, shape):
                V(lambda e: e.tensor_scalar(out=kt, in0=ang, scalar1=1.0 / TWO_PI, scalar2=MAGIC, op0=ALU.mult, op1=ALU.add))
                V(lambda e: e.tensor_scalar(out=kt, in0=kt, scalar1=MAGIC, scalar2=None, op0=ALU.subtract))
                V(lambda e: e.scalar_tensor_tensor(out=kt, in0=kt, scalar=-TWO_PI, in1=ang, op0=ALU.mult, op1=ALU.add))
                V(lambda e: e.activation(out=sin_out, in_=kt, func=AF.Sin), "act")
                V(lambda e: e.tensor_scalar(out=ang, in0=ang, scalar1=float(np.pi / 2), scalar2=None, op0=ALU.add))
                V(lambda e: e.tensor_scalar(out=kt, in0=ang, scalar1=1.0 / TWO_PI, scalar2=MAGIC, op0=ALU.mult, op1=ALU.add))
                V(lambda e: e.tensor_scalar(out=kt, in0=kt, scalar1=MAGIC, scalar2=None, op0=ALU.subtract))
                V(lambda e: e.scalar_tensor_tensor(out=kt, in0=kt, scalar=-TWO_PI, in1=ang, op0=ALU.mult, op1=ALU.add))
                V(lambda e: e.activation(out=cos_out, in_=kt, func=AF.Sin), "act")

            Er = tmp([128, 16, NE])
            Ei = tmp([128, 16, NE])
            sincos(ANG, KT, Ei, Er, None)
            V(lambda e: e.tensor_tensor(out=Er, in0=Er, in1=mag, op=ALU.mult))
            V(lambda e: e.tensor_tensor(out=Ei, in0=Ei, in1=mag, op=ALU.mult))
            Eis = tmp([128, 16, NE])
            Eisn = tmp([128, 16, NE])
            Ein = tmp([128, 16, NE])
            Ers2 = tmp([128, 16, NE])
            Ers0 = tmp([128, 16, NE])
            V(lambda e: e.tensor_scalar(out=Eis, in0=Ei, scalar1=sgn[:, 0:1], scalar2=None, op0=ALU.mult))
            V(lambda e: e.tensor_scalar(out=Eisn, in0=Ei, scalar1=sgn[:, 1:2], scalar2=None, op0=ALU.mult))
            V(lambda e: e.tensor_scalar(out=Ein, in0=Ei, scalar1=-1.0, scalar2=None, op0=ALU.mult))
            V(lambda e: e.tensor_scalar(out=Ers2, in0=Er, scalar1=sgn[:, 1:2], scalar2=None, op0=ALU.mult))
            V(lambda e: e.tensor_scalar(out=Ers0, in0=Er, scalar1=sgn[:, 0:1], scalar2=None, op0=ALU.mult))
            nr = tmp([128, 16])
            t1 = tmp([128, 16])
            t2 = tmp([128, 16])
            den = tmp([128, 16])
            cr = tmp([128, 16, 1])
            ci = tmp([128, 16, 1])
            cis = tmp([128, 16, 1])
            cisn = tmp([128, 16, 1])
            V(lambda e: e.tensor_scalar(out=nr, in0=Er[:, :, 17], scalar1=-1.0, scalar2=None, op0=ALU.add))
            V(lambda e: e.tensor_tensor(out=t1, in0=A[:, 0], in1=A[:, 0], op=ALU.mult))
            V(lambda e: e.tensor_tensor(out=t2, in0=A[:, 1], in1=A[:, 1], op=ALU.mult))
            V(lambda e: e.tensor_tensor(out=den, in0=t1, in1=t2, op=ALU.add))
            V(lambda e: e.reciprocal(out=den, in_=den))
            V(lambda e: e.tensor_tensor(out=t1, in0=nr, in1=A[:, 0], op=ALU.mult))
            V(lambda e: e.tensor_tensor(out=t2, in0=Ei[:, :, 17], in1=A[:, 1], op=ALU.mult))
            V(lambda e: e.tensor_tensor(out=t1, in0=t1, in1=t2, op=ALU.add))
            V(lambda e: e.tensor_tensor(out=cr[:, :, 0], in0=t1, in1=den, op=ALU.mult))
            V(lambda e: e.tensor_tensor(out=t1, in0=Ei[:, :, 17], in1=A[:, 0], op=ALU.mult))
            V(lambda e: e.tensor_tensor(out=t2, in0=nr, in1=A[:, 1], op=ALU.mult))
            V(lambda e: e.tensor_tensor(out=t1, in0=t1, in1=t2, op=ALU.subtract))
            V(lambda e: e.tensor_tensor(out=ci[:, :, 0], in0=t1, in1=den, op=ALU.mult))
            V(lambda e: e.tensor_scalar(out=cis[:, :, 0], in0=ci[:, :, 0], scalar1=sgn[:, 0:1], scalar2=None, op0=ALU.mult))
            V(lambda e: e.tensor_scalar(out=cisn[:, :, 0], in0=ci[:, :, 0], scalar1=sgn[:, 1:2], scalar2=None, op0=ALU.mult))
            Bb = tmp([128, 16, 16])
            Bbw = tmp([128, 16, 16])
            tt = tmp([128, 16, 16])
            bc_h = lambda t: t.to_broadcast([128, 16, 16])
            V(lambda e: e.tensor_tensor(out=Bb, in0=bc_h(cr), in1=Bp[:, 0], op=ALU.mult))
            V(lambda e: e.tensor_tensor(out=tt, in0=bc_h(cis), in1=Bp[:, 1], op=ALU.mult))
            V(lambda e: e.tensor_tensor(out=Bb, in0=Bb, in1=tt, op=ALU.add))
            V(lambda e: e.tensor_tensor(out=Bbw, in0=bc_h(cr), in1=Bp[:, 1], op=ALU.mult))
            V(lambda e: e.tensor_tensor(out=tt, in0=bc_h(cisn), in1=Bp[:, 0], op=ALU.mult))
            V(lambda e: e.tensor_tensor(out=Bbw, in0=Bbw, in1=tt, op=ALU.add))
            TT = tmp([128, 16, 8, 16])

            def table(E1, e0, B1, E2, B2):
                out = tmp([128, 16, 8, 16])
                e1 = E1[:, :, e0:e0 + 8].unsqueeze(3).to_broadcast([128, 16, 8, 16])
                e2 = E2[:, :, e0:e0 + 8].unsqueeze(3).to_broadcast([128, 16, 8, 16])
                b1 = B1.unsqueeze(2).to_broadcast([128, 16, 8, 16])
                b2 = B2.unsqueeze(2).to_broadcast([128, 16, 8, 16])
                V(lambda e: e.tensor_tensor(out=out, in0=e1, in1=b1, op=ALU.mult))
                V(lambda e: e.tensor_tensor(out=TT, in0=e2, in1=b2, op=ALU.mult))
                V(lambda e: e.tensor_tensor(out=out, in0=out, in1=TT, op=ALU.add))
                return out

            Bneg = table(Er, 0, Bb, Eis, Bbw)
            B7 = table(Er, 8, Bb, Eis, Bbw)
            B7x = table(Er, 8, Bbw, Eisn, Bb)
            Cpos = table(Ers2, 16, Cp[:, 0], Ein, Cp[:, 1])
            Cx = table(Ers2, 17, Cp[:, 0], Ein, Cp[:, 1])
            Cxx = table(Ers0, 17, Cp[:, 1], Ein, Cp[:, 0])
            TBs = tmp([128, 16, 5, 128], BF16)
            fl = lambda t, g: t[:, g].rearrange("p s h -> p (s h)")
            V(lambda e: e.tensor_copy(out=TBs[:, :, 2, :], in_=Cx.rearrange("p g s h -> p g (s h)")))
            V(lambda e: e.tensor_copy(out=TBs[:, :, 3, :], in_=Cxx.rearrange("p g s h -> p g (s h)")))
            mtmp = tmp([128, 128])
            for g in range(16):
                bk = g % 2
                P.op("pe", lambda e, g=g, bk=bk: e.matmul(PS[bk][:, 0:128], lhsT=fl(Bneg, g), rhs=fl(Cpos, g), start=True, stop=True),
                     reads=[pb], writes=[PSB[bk]])
                P.op("pe", lambda e, g=g, bk=bk: e.transpose(out=PS[bk][:, 128:256], in_=fl(B7, g), identity=identf),
                     reads=[pb, cB], writes=[PSB[bk]])
                P.op("pe", lambda e, g=g, bk=bk: e.transpose(out=PS[bk][:, 256:384], in_=fl(B7x, g), identity=identf),
                     reads=[pb, cB], writes=[PSB[bk]])
                tb2 = P.buf()
                P.op("dve", lambda e, bk=bk: e.tensor_tensor(out=mtmp, in0=PS[bk][:, 0:128], in1=mask8, op=ALU.mult),
                     reads=[PSB[bk], b0, pb], writes=[pb])
                P.op("dve", lambda e, g=g: e.scalar_tensor_tensor(out=TBs[:, g, 4, :], in0=identf, scalar=dcol[:, g:g + 1], in1=mtmp,
                                                                    op0=ALU.mult, op1=ALU.add), reads=[pb, cB, b0], writes=[pb])
                P.op("act", lambda e, g=g, bk=bk: e.activation(out=TBs[:, g, 0:2, :], in_=PS[bk][:, 128:384].rearrange("p (a b) -> p a b", a=2), func=AF.Copy),
                     reads=[PSB[bk], pb], writes=[pb])
            P.dma("sp", TB_d[l], TBs.rearrange("p g a b -> p (g a b)"), reads=[pb], writes=[TBdB[l]])
            phi = tmp([128, 16, 1])
            kt1 = tmp([128, 16, 1])
            V(lambda e: e.tensor_scalar(out=phi, in0=b_, scalar1=8.0, scalar2=None, op0=ALU.mult))
            V(lambda e: e.tensor_scalar(out=kt1, in0=phi, scalar1=1.0 / TWO_PI, scalar2=MAGIC, op0=ALU.mult, op1=ALU.add))
            V(lambda e: e.tensor_scalar(out=kt1, in0=kt1, scalar1=MAGIC, scalar2=None, op0=ALU.subtract))
            V(lambda e: e.scalar_tensor_tensor(out=phi, in0=kt1, scalar=-TWO_PI, in1=phi, op0=ALU.mult, op1=ALU.add))
            o[0] = 0
            keep = carve(190000, [128, 2, 16], F32)
            V(lambda e: e.tensor_copy(out=keep[:, 0], in_=phi[:, :, 0]))
            V(lambda e: e.activation(out=rho[:, l], in_=a_[:, :, 0], func=AF.Exp, scale=8.0), "act")
            cidx2 = carve(192000, [128, 256], F32)
            V(lambda e: e.tensor_copy(out=cidx2, in_=cidx))
            ANG2 = tmp([128, 16, 256])
            KT2 = tmp([128, 16, 256])
            ROTs = tmp([128, 16, 2, 256])
            V(lambda e: e.tensor_tensor(out=ANG2, in0=keep[:, 0].unsqueeze(2).to_broadcast([128, 16, 256]),
                                        in1=cidx2.unsqueeze(1).to_broadcast([128, 16, 256]), op=ALU.mult))
            sincos(ANG2, KT2, ROTs[:, :, 1, :], ROTs[:, :, 0, :], None)
            V(lambda e: e.tensor_scalar(out=ROTs[64:128, :, 1, :], in0=ROTs[64:128, :, 1, :], scalar1=-1.0, scalar2=None, op0=ALU.mult))
            P.dma("sp", ROT_d[l], ROTs.rearrange("p g a b -> p (g a b)"), reads=[pb], writes=[ROTdB[l]])

        TBdB = P.bufs(2, "TBd")
        ROTdB = P.bufs(2, "ROTd")
        for l in range(2):
            prologue(l)
            P.barrier()
        if stop == "prologue":
            tap("rho", rho, cB)
            P.finish()
            P.emit()
            return nc, tap_out

        w_in_v = [w_in_d[l].rearrange("(k p) c -> p k c", p=128) for l in range(2)]
        w_out_v = [w_out_d[l].rearrange("(k p) c -> p k c", p=128) for l in range(2)]
        hT5 = hT.rearrange("p k (hf c s) -> p k hf c s", hf=2, s=8)
        wsel = [0]

        def attn_pieces(base, which):
            pcs = []
            off0 = 0 if which == "A" else 640
            for s_ in range(3):
                pcs.append((base + off0 + s_ * 64, 64, s_ * 128))
                pcs.append((base + off0 + (3 + s_) * 64, 64, s_ * 128 + 64))
            pcs.append((base + (384 if which == "A" else 512), 128, 384))
            return pcs

        def mk_in(l, pieces):
            def issue(wi):
                for (c0, n, off) in pieces:
                    P.dma("pool", Wb[wi][:, :, off:off + n], w_in_v[l][:, :, c0:c0 + n], writes=[WbB[wi]])
            return issue

        def mk_out(l, hfc):
            def issue(wi):
                c0 = hfc * 512
                P.dma("pool", Wb[wi][:, 0:2, :], w_out_v[l][:, 0:2, c0:c0 + 512], writes=[WbB[wi]])
                for br in range(2):
                    for s_ in range(3):
                        k = 2 + br * 3 + s_
                        ra = 256 + br * 384 + s_ * 64
                        rb = 256 + br * 384 + (3 + s_) * 64
                        P.dma("pool", Wb[wi][0:64, k, :], w_out_d[l][ra:ra + 64, c0:c0 + 512], writes=[WbB[wi]])
                        P.dma("pool", Wb[wi][64:128, k, :], w_out_d[l][rb:rb + 64, c0:c0 + 512], writes=[WbB[wi]])
            return issue

        WQ = []
        for s__ in range(nseq):
            for l__ in range(2):
                WQ.append(mk_in(l__, [(0, 512, 0)]))
                for base__ in (512, 1536):
                    WQ.append(mk_in(l__, attn_pieces(base__, "A")))
                    WQ.append(mk_in(l__, attn_pieces(base__, "B")))
                WQ.append(mk_out(l__, 0))
                WQ.append(mk_out(l__, 1))
        wq_pos = [0]
        wq_issued = [0]

        def get_w(prefetch=True):
            n = wq_pos[0]
            wq_pos[0] += 1
            for m in ((n, n + 1) if prefetch else (n,)):
                if m < len(WQ) and m >= wq_issued[0]:
                    WQ[m](m % 2)
                    wq_issued[0] = m + 1
            return n % 2

        def prefetch_w(ahead=0):
            m = wq_pos[0] + ahead
            if m < len(WQ) and m == wq_issued[0]:
                WQ[m](m % 2)
                wq_issued[0] = m + 1

        pbank = [0]

        def proj_fm(wi, col_off, n, bank_pool=(0, 1)):
            bk = bank_pool[pbank[0] % len(bank_pool)]
            pbank[0] += 1
            for k in range(8):
                P.op("pe", lambda e, k=k, bk=bk: e.matmul(PS[bk][:, :], lhsT=Wb[wi][:, k, col_off:col_off + 128],
                                                          rhs=hT[:, k, n * 512:(n + 1) * 512], start=(k == 0), stop=(k == 7)),
                     reads=[WbB[wi], hTB[n]], writes=[PSB[bk]])
            return bk

        def norm_phase(l):
            junk = hT[:, 7, 0:1024]
            for i in range(NT):
                P.op("act", lambda e, i=i: e.activation(out=junk, in_=X[:, i, :], func=AF.Square, accum_out=ssq[:, i:i + 1]),
                     reads=[XB[i]], writes=[hTB[0], hTB[1], ssqB])
            P.op("act", lambda e: e.activation(out=srt, in_=ssq, func=AF.Sqrt, scale=1.0 / D, bias=epsb[:, 0:1]), reads=[ssqB, cB], writes=[srtB])
            P.op("dve", lambda e: e.reciprocal(out=rstd, in_=srt), reads=[srtB], writes=[rstdB])
            for i in range(NT):
                j = i % 2
                P.op("act", lambda e, i=i, j=j: e.activation(out=hb[j], in_=X[:, i, :], func=AF.Copy, scale=rstd[:, i:i + 1]),
                     reads=[XB[i], rstdB], writes=[hbB[j]])
                tp = PS[6 + j].bitcast(BF16).rearrange("p (k t) -> p k t", k=8)
                for k in range(8):
                    P.op("pe", lambda e, j=j, k=k, tp=tp: e.transpose(out=tp[:, k, :], in_=hb[j][:, k * 128:(k + 1) * 128], identity=identb),
                         reads=[hbB[j], cB], writes=[PSB[6 + j]])
                P.op("dve", lambda e, i=i, tp=tp: e.tensor_tensor(out=hT[:, :, i * 128:(i + 1) * 128], in0=tp,
                                                                  in1=normg[:, l].to_broadcast([128, 8, 128]), op=ALU.mult),
                     reads=[PSB[6 + j], cB], writes=[hTB[i // 4]])

        epsb = pers([128, 1], F32) if False else carve(T0 - 16, [128, 1], F32)
        P.op("pool", lambda e: e.memset(epsb, EPS), writes=[cB])

        def ssm_phase(l):
            o = [T0]

            def tmp(shape, dtype=F32):
                n = _prod(shape[1:]) * (4 if dtype == F32 else 2)
                a = carve(o[0], shape, dtype)
                o[0] += (n + 3) // 4 * 4
                return a

            Utok = tmp([128, 2, 8, 256], BF16)
            Utok5 = Utok.rearrange("p hf (g s) (a h) -> p hf (g a) s h", g=8, a=2) if False else Utok.rearrange("p hf s x -> p hf (s x)").rearrange("p hf (g s h) -> p hf g s h", g=16, s=8)
            UtokB = pbuf("ssm.Utok")
            Ug = tmp([128, 16, 256], BF16)
            UgB = pbufs(16, "ssm.Ug")
            Yg = tmp([128, 16, 256], BF16)
            YgB = pbufs(16, "ssm.Yg")
            TBb = [tmp([128, 2, 5, 128], BF16) for _ in range(2)]
            TBbB = pbufs(3, "ssm.b.TBb")
            ROTb = [tmp([128, 2, 2, 256], F32) for _ in range(2)]
            ROTbB = pbufs(3, "ssm.b.ROTb")
            TBb.append(carve(T0, [128, 2, 5, 128], BF16))
            ROTb.append(carve(T0 + 2560, [128, 2, 2, 256], F32))
            T1 = [tmp([128, 2, 256], F32) for _ in range(2)]
            T2 = [tmp([128, 2, 256], F32) for _ in range(2)]
            T1B, T2B = pbufs(2, "ssm.b.T1"), pbufs(2, "ssm.b.T2")
            Pb = [tmp([128, 2, 258], BF16) for _ in range(2)]
            Qb = [tmp([128, 2, 258], BF16) for _ in range(2)]
            PQB = pbufs(2, "ssm.b.PQ")
            ypB = pbufs(2, "ssm.e.yp")
            w1Bs = pbufs(2, "ssm.e.w1")
            yglB = pbufs(2, "ssm.e.ygl")
            sg2Bs = pbufs(2, "ssm.e.sg2")
            fence(group("att."), [b for b in group("ssm.") if not b.name.startswith("ssm.e.")])
            assert o[0] <= ARENA * 4, o[0]
            wi = get_w()
            for c in range(2):
                for n in range(4):
                    bk = proj_fm(wi, 256 + c * 128, n)
                    P.op("act", lambda e, c=c, n=n, bk=bk: e.activation(out=yT[:, c, n * 512:(n + 1) * 512], in_=PS[bk], func=AF.Silu),
                         reads=[PSB[bk]], writes=[yTB[(c, n)]])
            for hf in range(2):
                for s in range(8):
                    bk = (hf * 8 + s) % 2 + 2
                    for k in range(8):
                        P.op("pe", lambda e, k=k, bk=bk, hf=hf, s=s: e.matmul(PS[bk][:, 0:256], lhsT=hT5[:, k, hf, :, s], rhs=Wb[wi][:, k, 0:256],
                                                                             start=(k == 0), stop=(k == 7)),
                             reads=[WbB[wi], hTB[2 * hf], hTB[2 * hf + 1]], writes=[PSB[bk]])
                    P.op("act", lambda e, bk=bk, hf=hf, s=s: e.activation(out=Utok5[:, hf, :, s, :], in_=PS[bk][:, 0:256].rearrange("p (g h) -> p g h", g=16), func=AF.Copy),
                         reads=[PSB[bk]], writes=[UtokB])
            prefetch_w(1)
            for j_ in range(2):
                P.op("pool", lambda e, j_=j_: e.memset(Pb[j_], 0.0), writes=[PQB[j_]])
                P.op("pool", lambda e, j_=j_: e.memset(Qb[j_], 0.0), writes=[PQB[j_]])
            for hf in range(2):
                for g8 in range(2):
                    bk = 6 + (hf * 2 + g8) % 2
                    tp = PS[bk].bitcast(BF16).rearrange("p (g c) -> p g c", g=8)
                    for gi in range(8):
                        g = g8 * 8 + gi
                        P.op("pe", lambda e, tp=tp, gi=gi, g=g, hf=hf: e.transpose(out=tp[:, gi, :], in_=Utok5[:, hf, g].rearrange("p s h -> p (s h)"), identity=identb),
                             reads=[UtokB, cB], writes=[PSB[bk]])
                    P.op("dve", lambda e, tp=tp, g8=g8, hf=hf: e.tensor_copy(out=Ug[:, g8 * 8:(g8 + 1) * 8, hf * 128:(hf + 1) * 128], in_=tp),
                         reads=[PSB[bk]], writes=UgB[g8 * 8:(g8 + 1) * 8])
            tap("Ug", Ug, UgB)
            TBv = TB_d[l].rearrange("p (g a b) -> p g a b", g=16, a=5)
            ROTv = ROT_d[l].rearrange("p (g a b) -> p g a b", g=16, a=2)
            def ld_tab(bt):
                j3 = bt % 3
                g0 = bt * 2
                P.dma("sp", TBb[j3], TBv[:, g0:g0 + 2], reads=[TBdB[l]], writes=[TBbB[j3]])
                P.dma("sp", ROTb[j3], ROTv[:, g0:g0 + 2], reads=[ROTdB[l]], writes=[ROTbB[j3]])

            def s1(bt):
                j = bt % 2
                j3 = bt % 3
                g0 = bt * 2
                b0_, b1_ = 2 * j, 2 * j + 1
                for gi in range(2):
                    g = g0 + gi
                    P.op("pe", lambda e, gi=gi, g=g: e.matmul(PS[b0_][:, gi * 256:(gi + 1) * 256], lhsT=TBb[j3][:, gi, 0, :], rhs=Ug[:, g, :], start=True, stop=True),
                         reads=[TBbB[j3], UgB[g]], writes=[PSB[b0_]])
                    P.op("pe", lambda e, gi=gi, g=g: e.matmul(PS[b1_][:, gi * 256:(gi + 1) * 256], lhsT=TBb[j3][:, gi, 1, :], rhs=Ug[:, g, :], start=True, stop=True),
                         reads=[TBbB[j3], UgB[g]], writes=[PSB[b1_]])

            def s2(bt):
                j = bt % 2
                j3 = bt % 3
                g0 = bt * 2
                b0_, b1_ = 2 * j, 2 * j + 1
                ps0 = PS[b0_].rearrange("p (g c) -> p g c", g=2)
                ps1 = PS[b1_].rearrange("p (g c) -> p g c", g=2)
                P.op("dve", lambda e: e.tensor_tensor(out=T1[j], in0=ps0, in1=ROTb[j3][:, :, 0, :], op=ALU.mult),
                     reads=[PSB[b0_], ROTbB[j3]], writes=[T1B[j]])
                P.op("dve", lambda e: e.tensor_tensor(out=T2[j], in0=ps1, in1=ROTb[j3][:, :, 1, :], op=ALU.mult),
                     reads=[PSB[b1_], ROTbB[j3]], writes=[T2B[j]])
                P.op("dve", lambda e: e.tensor_tensor(out=T1[j], in0=T1[j], in1=T2[j], op=ALU.add), reads=[T1B[j], T2B[j]], writes=[T1B[j]])
                for gi in range(2):
                    g = g0 + gi
                    P.op("dve", lambda e, gi=gi, g=g: e.tensor_tensor_scan(out=T2[j][:, gi, :], data0=rho[:, l, g:g + 1].to_broadcast([128, 256]),
                                                                          data1=T1[j][:, gi, :], initial=0.0, op0=ALU.mult, op1=ALU.add),
                         reads=[T1B[j], cB], writes=[T2B[j]])
                P.op("pool", lambda e: e.tensor_tensor(out=Pb[j][:, :, 1:257], in0=T2[j], in1=ROTb[j3][:, :, 0, :], op=ALU.mult),
                     reads=[T2B[j], ROTbB[j3]], writes=[PQB[j]])
                P.op("pool", lambda e: e.tensor_tensor(out=Qb[j][:, :, 1:257], in0=T2[j], in1=ROTb[j3][:, :, 1, :], op=ALU.mult),
                     reads=[T2B[j], ROTbB[j3]], writes=[PQB[j]])

            def s3(bt):
                j = bt % 2
                j3 = bt % 3
                g0 = bt * 2
                yb = 4 + bt % 2
                for gi in range(2):
                    g = g0 + gi
                    yo = PS[yb][:, gi * 256:(gi + 1) * 256]
                    P.op("pe", lambda e, gi=gi, g=g, yo=yo: e.matmul(yo, lhsT=TBb[j3][:, gi, 4, :], rhs=Ug[:, g, :], start=True, stop=False),
                         reads=[TBbB[j3], UgB[g]], writes=[PSB[yb]])
                    P.op("pe", lambda e, gi=gi, yo=yo: e.matmul(yo, lhsT=TBb[j3][:, gi, 2, :], rhs=Pb[j][:, gi, 0:256], start=False, stop=False),
                         reads=[TBbB[j3], PQB[j]], writes=[PSB[yb]])
                    P.op("pe", lambda e, gi=gi, yo=yo: e.matmul(yo, lhsT=TBb[j3][:, gi, 3, :], rhs=Qb[j][:, gi, 0:256], start=False, stop=True),
                         reads=[TBbB[j3], PQB[j]], writes=[PSB[yb]])
                P.op("act", lambda e: e.activation(out=Yg[:, g0:g0 + 2, :], in_=PS[yb].rearrange("p (g c) -> p g c", g=2), func=AF.Copy),
                     reads=[PSB[yb]], writes=YgB[g0:g0 + 2])

            fence([UtokB], [TBbB[2], ROTbB[2]])
            ld_tab(0)
            ld_tab(1)
            ld_tab(2)
            s1(0)
            for bt in range(8):
                s2(bt)
                if bt + 1 < 8:
                    s1(bt + 1)
                s3(bt)
                if bt + 3 < 8:
                    ld_tab(bt + 3)
            tap("Yg", Yg, YgB)
            fence([TBbB[2], ROTbB[2]], [UtokB])
            Ytok = Utok
            for hf in range(2):
                for g8 in range(2):
                    bk = 6 + (hf * 2 + g8) % 2
                    tp = PS[bk].bitcast(BF16).rearrange("p (g t h) -> p g t h", g=8, t=8)
                    tpf = PS[bk].bitcast(BF16).rearrange("p (g c) -> p g c", g=8)
                    for gi in range(8):
                        g = g8 * 8 + gi
                        P.op("pe", lambda e, tpf=tpf, gi=gi, g=g, hf=hf: e.transpose(out=tpf[:, gi, :], in_=Yg[:, g, hf * 128:(hf + 1) * 128], identity=identb),
                             reads=[YgB[g], cB], writes=[PSB[bk]])
                    P.op("dve", lambda e, tp=tp, g8=g8, hf=hf: e.tensor_copy(
                        out=Ytok[:, hf, :, g8 * 128:(g8 + 1) * 128].rearrange("p t (g h) -> p g t h", g=8), in_=tp),
                        reads=[PSB[bk]], writes=[UtokB])
            o2 = [o[0] - 0]
            o[0] = T0 + 8192 + 8192 + 8192
            fence(group("ssm.b."), group("ssm.e."))
            yp = [tmp([128, 8, 128], BF16) for _ in range(2)]
            w1s = [tmp([128, 1024], F32) for _ in range(2)]
            ygl = [tmp([128, 1024], BF16) for _ in range(2)]
            sg2s = [tmp([128, 512], F32) for _ in range(2)]
            assert o[0] <= ARENA * 4
            gcnt = 0
            for hf in range(2):
                for ch in range(2):
                    bk = 6 + ch
                    w1, w1B = w1s[ch], w1Bs[ch]
                    tp = PS[bk].bitcast(BF16).rearrange("p (t c) -> p t c", t=8)
                    for t in range(8):
                        P.op("pe", lambda e, tp=tp, t=t, ch=ch, hf=hf: e.transpose(out=tp[:, t, :], in_=Ytok[:, hf, t, ch * 128:(ch + 1) * 128], identity=identb),
                             reads=[UtokB, cB], writes=[PSB[bk]])
                    P.op("act", lambda e, tp=tp, ch=ch: e.activation(out=yp[ch], in_=tp, func=AF.Copy), reads=[PSB[bk]], writes=[ypB[ch]])
                    ypf = yp[ch].rearrange("p t c -> p (t c)")
                    P.op("dve", lambda e, ypf=ypf, w1=w1: e.tensor_tensor(out=w1, in0=ypf, in1=ypf, op=ALU.mult), reads=[ypB[ch]], writes=[w1B])
                    P.op("dve", lambda e, w1=w1: e.tensor_scalar(out=w1, in0=w1, scalar1=0.044715, scalar2=1.0, op0=ALU.mult, op1=ALU.add), reads=[w1B], writes=[w1B])
                    P.op("dve", lambda e, ypf=ypf, w1=w1: e.tensor_tensor(out=w1, in0=w1, in1=ypf, op=ALU.mult), reads=[w1B, ypB[ch]], writes=[w1B])
                    P.op("act", lambda e, w1=w1: e.activation(out=w1, in_=w1, func=AF.Sigmoid, scale=1.5957691216057308), reads=[w1B], writes=[w1B])
                    P.op("dve", lambda e, ypf=ypf, ch=ch, w1=w1: e.tensor_tensor(out=ygl[ch], in0=w1, in1=ypf, op=ALU.mult), reads=[w1B, ypB[ch]], writes=[yglB[ch]])
                for f in range(2):
                    for hh in range(2):
                        bk = (f * 2 + hh) % 2
                        sg2, sg2B = sg2s[gcnt % 2], sg2Bs[gcnt % 2]
                        gcnt += 1
                        for ec in range(2):
                            P.op("pe", lambda e, ec=ec, f=f, hh=hh, bk=bk: e.matmul(PS[bk], lhsT=gluw[:, l, ec, f * 128:(f + 1) * 128],
                                                                                   rhs=ygl[ec][:, hh * 512:(hh + 1) * 512], start=(ec == 0), stop=(ec == 1)),
                                 reads=[yglB[0], yglB[1], cB], writes=[PSB[bk]])
                        P.op("act", lambda e, bk=bk, f=f, sg2=sg2: e.activation(out=sg2, in_=PS[bk], func=AF.Sigmoid, bias=glub[:, l, f:f + 1]),
                             reads=[PSB[bk], cB], writes=[sg2B])
                        P.op("dve", lambda e, f=f, hh=hh, sg2=sg2: e.tensor_tensor(out=sg2, in0=sg2, in1=ygl[f][:, hh * 512:(hh + 1) * 512], op=ALU.mult),
                             reads=[sg2B, yglB[f]], writes=[sg2B])
                        dst = yT[:, f, hf * 1024:(hf + 1) * 1024].rearrange("p (c t) -> p t c", t=8)[:, 4 * hh:4 * hh + 4, :]
                        P.op("pool", lambda e, dst=dst, sg2=sg2: e.tensor_tensor(out=dst, in0=sg2.rearrange("p (t c) -> p t c", t=4), in1=dst, op=ALU.mult),
                             reads=[sg2B], writes=[yTB[(f, 2 * hf)], yTB[(f, 2 * hf + 1)]])
            tap("yssm", yT[:, 0:2, :], [yTB[(c, n)] for c in range(2) for n in range(4)])

        def head_norm1(bk, tmpb):
            sqb, srtb, rinvb, sqB, srtB2, rinvB = tmpb
            P.op("act", lambda e: e.activation(out=sqb, in_=PS[bk], func=AF.Square), reads=[PSB[bk]], writes=[sqB])

        def head_norm2(bk, l, idx, dst, dstB, tmpb, b2):
            sqb, srtb, rinvb, sqB, srtB2, rinvB = tmpb
            P.op("pe", lambda e: e.matmul(PS[b2], lhsT=bones, rhs=sqb, start=True, stop=True), reads=[sqB, cB], writes=[PSB[b2]])
            P.op("act", lambda e: e.activation(out=rinvb, in_=PS[b2], func=AF.Ln, scale=1.0 / 64, bias=epsb[:, 0:1]), reads=[PSB[b2], cB], writes=[rinvB])
            P.op("act", lambda e: e.activation(out=rinvb, in_=rinvb, func=AF.Exp, scale=-0.5), reads=[rinvB], writes=[rinvB])
            if isinstance(dst, tuple):
                kTz, n = dst[2], dst[1]
                for hv in range(2):
                    H = slice(hv * 64, hv * 64 + 64)
                    P.op("dve", lambda e, H=H, hv=hv: e.scalar_tensor_tensor(out=kTz[H, hv, n * 512:(n + 1) * 512], in0=PS[bk][H], scalar=hn[H, l, idx:idx + 1], in1=rinvb[H],
                                                                         op0=ALU.mult, op1=ALU.mult),
                         reads=[PSB[bk], rinvB, cB], writes=[dstB])
                return
            P.op("dve", lambda e: e.scalar_tensor_tensor(out=dst, in0=PS[bk], scalar=hn[:, l, idx:idx + 1], in1=rinvb, op0=ALU.mult, op1=ALU.mult),
                 reads=[PSB[bk], rinvB, cB], writes=[dstB])

        def attn_phase(l, moba):
            base = 1536 if moba else 512
            ych = 5 if moba else 2
            o = [T0]

            def tmp(shape, dtype=F32, parts=128):
                n = _prod(shape[1:]) * (4 if dtype == F32 else 2)
                a = carve(o[0], shape, dtype, parts)
                o[0] += (n + 3) // 4 * 4
                return a

            qT = tmp([128, 3, 2048], BF16)
            qTB = {(c, n): pbuf(f"att.q{c}_{n}") for c in range(3) for n in range(4)}
            kT = tmp([128, 2, 2048], BF16)
            kTB = {n: pbuf(f"att.k{n}") for n in range(4)}
            Vd = tmp([128, 16, 2, 192], BF16)
            VdB = pbufs(16, "att.Vd")
            PtD = [tmp([128, 2, 384], BF16) for _ in range(4)]
            PtDB = pbufs(4, "att.Pt")
            rr1 = tmp([128, 384], F32)
            rr = [rr1, rr1]
            rrB = pbuf("att.rr")
            oo1 = tmp([128, 384], F32)
            oo = [oo1, oo1]
            ooB = pbuf("att.oo")
            kmB = pbuf("att.km")
            if moba:
                kmf = tmp([128, 2, 8], F32)
                kmT = tmp([128, 2, 8], BF16)
            R_start = o[0]
            hnt = []
            for _ in range(2):
                a1, a3 = tmp([128, 512], BF16), tmp([128, 512], F32)
                hnt.append((a1, None, a3, pbuf(f"att.hn{_}a"), pbuf(f"att.hn{_}b"), pbuf(f"att.hn{_}c")))
            R_end = max(o[0], R_start + 9216)
            assert R_end <= ARENA * 4, R_end
            selB = pbuf("att.sel")
            nsB = pbuf("att.ns_all")
            ndB = pbufs(2, "att.nd")
            nd = [carve(R_start + 6144, [128, 384], F32), carve(R_start + 6144 + 1536, [128, 384], F32)]
            hnB_all = [b for t_ in hnt for b in t_[3:]]
            if not moba:
                fence(group("ssm."), group("att."))
            qi, ki = (2, 3) if moba else (0, 1)
            P.op("pool", lambda e: e.memset(Vd[:, :, :, 0:64], 1.0), writes=VdB)
            P.op("pool", lambda e: e.memset(Vd[:, :, :, 128:192], 1.0), writes=VdB)
            P.op("pool", lambda e: e.memset(kT[64:128, 0, :], 0.0), writes=[kTB[n] for n in range(4)])
            P.op("pool", lambda e: e.memset(kT[0:64, 1, :], 0.0), writes=[kTB[n] for n in range(4)])
            wi = get_w()
            pend = []
            cnt = 0

            def hn_tile(wi, col_off, n, idx, dst, dstB):
                nonlocal cnt
                bk = proj_fm(wi, col_off, n, bank_pool=(0, 1, 2))
                t = hnt[cnt % 2]
                b2 = 4 + cnt % 2
                cnt += 1
                head_norm1(bk, t)
                pend.append((bk, idx, dst, dstB, t, b2))
                if len(pend) > 1:
                    a_ = pend.pop(0)
                    head_norm2(a_[0], l, a_[1], a_[2], a_[3], a_[4], a_[5])

            def hn_flush():
                while pend:
                    a_ = pend.pop(0)
                    head_norm2(a_[0], l, a_[1], a_[2], a_[3], a_[4], a_[5])

            for n in range(4):
                hn_tile(wi, 384, n, ki, ("k", n, kT), kTB[n])
            for c in range(3):
                for n in range(4):
                    hn_tile(wi, c * 128, n, qi, qT[:, c, n * 512:(n + 1) * 512], qTB[(c, n)])
            hn_flush()
            wi = get_w()
            for i in range(NT):
                bk = 6 + i % 2
                for k in range(8):
                    P.op("pe", lambda e, k=k, bk=bk, i=i: e.matmul(PS[bk][:, 0:128], lhsT=hT[:, k, i * 128:(i + 1) * 128], rhs=Wb[wi][:, k, 384:512],
                                                                  start=(k == 0), stop=(k == 7)),
                         reads=[WbB[wi], hTB[i // 4]], writes=[PSB[bk]])
                P.op("act", lambda e, bk=bk, i=i: e.activation(out=Vd[:, i, :, 64:128], in_=PS[bk][:, 0:128].rearrange("p (a b) -> p a b", a=2), func=AF.Copy),
                     reads=[PSB[bk]], writes=[VdB[i]])
            for c in range(3):
                for n in range(4):
                    bk = proj_fm(wi, c * 128, n, bank_pool=(0, 1, 2))
                    P.op("act", lambda e, c=c, n=n, bk=bk: e.activation(out=yT[:, ych + c, n * 512:(n + 1) * 512], in_=PS[bk], func=AF.Silu),
                         reads=[PSB[bk]], writes=[yTB[(ych + c, n)]])
            prefetch_w(1)
            if stop == "attn_proj":
                return

            if moba:
                o[0] = R_start
                ns_all = tmp([128, 8, 2, 128], BF16)
                nsel = tmp([128, 8, 2, 128], BF16)
                gm = tmp([128, 8, 2, 8], F32)
                top8 = tmp([128, 8, 2, 8], F32)
                assert o[0] <= R_end
                fence(hnB_all + ndB, [selB, nsB])
                P.op("dve", lambda e: e.tensor_reduce(out=kmf, in_=kT.rearrange("p a (n t) -> p a n t", n=8), axis=AX.X, op=ALU.add),
                     reads=[kTB[n] for n in range(4)], writes=[kmB])
                P.op("dve", lambda e: e.tensor_scalar(out=kmT, in0=kmf, scalar1=1.0 / 256, scalar2=None, op0=ALU.mult), reads=[kmB], writes=[kmB])
                P.op("pool", lambda e: e.memset(nsel, 0.0), writes=[selB])
                gps = [PS[6][:, 0:64].rearrange("p (i b) -> p i b", i=8), PS[7][:, 0:64].rearrange("p (i b) -> p i b", i=8)]
                for ii in range(8):
                    i = 8 + ii
                    for kv in range(2):
                        for s_ in range(3):
                            P.op("pe", lambda e, kv=kv, i=i, ii=ii, s_=s_: e.matmul(
                                gps[kv][:, ii, :], lhsT=qT[:, s_, i * 128:(i + 1) * 128], rhs=kmT[:, kv, :], start=(s_ == 0), stop=(s_ == 2)),
                                reads=[qTB[(s_, i // 4)], kmB], writes=[PSB[6 + kv]])
                for kv in range(2):
                    P.op("dve", lambda e, kv=kv: e.tensor_tensor(out=gm[:, :, kv, :], in0=gps[kv], in1=bmask[:, 8:16, kv, :], op=ALU.add),
                         reads=[PSB[6 + kv], cB], writes=[selB])
                for ii in range(8):
                    for kv in range(2):
                        P.op("dve", lambda e, ii=ii, kv=kv: e.max(out=top8[:, ii, kv, :], in_=gm[:, ii, kv, :]), reads=[selB], writes=[selB])
                for ii in range(8):
                    for kv in range(2):
                        P.op("dve", lambda e, ii=ii, kv=kv: e.tensor_scalar(out=nsel[:, ii, kv, 0:8], in0=gm[:, ii, kv, :], scalar1=top8[:, ii, kv, 2:3], scalar2=None, op0=ALU.is_lt),
                             reads=[selB], writes=[selB])
            def prepass2():
                for bq in range(2):
                    tpn = PS[6 + bq].bitcast(BF16)
                    for i4 in range(4):
                        for kv in range(2):
                            ii = bq * 4 + i4
                            slot = i4 * 2 + kv
                            P.op("pe", lambda e, tpn=tpn, ii=ii, kv=kv, slot=slot: e.transpose(out=tpn[:, slot * 128:(slot + 1) * 128], in_=nsel[:, ii, kv, :], identity=identb),
                                 reads=[selB, cB], writes=[PSB[6 + bq]])
                    P.op("act", lambda e, tpn=tpn, bq=bq: e.activation(out=ns_all[:, bq * 4:(bq + 1) * 4], in_=tpn[:, 0:1024].rearrange("p (i a b) -> p i a b", i=4, a=2), func=AF.Copy),
                         reads=[PSB[6 + bq]], writes=[nsB])

            NP2 = 4
            pairs = []
            for i in range(NT):
                kts = list(range(0, i + 1)) if moba else ([i - 1, i] if i > 0 else [i])
                for jj, kt in enumerate(kts):
                    pairs.append(dict(i=i, kt=kt, first=(jj == 0), last=(jj == len(kts) - 1), n=len(pairs)))

            def stageA(p):
                i, kt, n_ = p["i"], p["kt"], p["n"]
                qb = i // 2
                nsb = 2
                sbig = PSbig[n_ % nsb]
                bks = [PSB[2 * (n_ % nsb)], PSB[2 * (n_ % nsb) + 1]]
                pt, ptB = PtD[n_ % NP2], PtDB[n_ % NP2]
                sel_here = moba and qb >= 4 and (kt // 2) < qb
                mk = None
                if kt == i:
                    mk = 0
                elif not moba:
                    mk = 1
                for kv in range(2):
                    so = sbig[:, kv * 512:kv * 512 + 384].rearrange("p (a b) -> p a b", a=3)
                    first = True
                    if mk is not None:
                        P.op("pe", lambda e, so=so, first=first: e.matmul(so, lhsT=identb, rhs=amask[:, mk, :].rearrange("p (a b) -> p a b", a=3), start=first, stop=False),
                             reads=[cB], writes=[bks[kv]])
                        first = False
                    if sel_here:
                        P.op("pe", lambda e, so=so, kv=kv, first=first: e.matmul(so, lhsT=kaug[:, kt // 2, :],
                                                                                rhs=ns_all[:, i - 8, kv:kv + 1, :].to_broadcast([128, 3, 128]), start=first, stop=False),
                             reads=[nsB, cB], writes=[bks[kv]])
                        first = False
                    P.op("pe", lambda e, so=so, kv=kv, first=first: e.matmul(so, lhsT=kT[:, kv, kt * 128:(kt + 1) * 128], rhs=qT[:, :, i * 128:(i + 1) * 128],
                                                                            start=first, stop=True),
                         reads=[kTB[kt // 4]] + [qTB[(c, i // 4)] for c in range(3)], writes=[bks[kv]])
                P.op("act", lambda e: e.activation(out=pt, in_=sbig.rearrange("p (k c) -> p k c", k=2)[:, :, 0:384], func=AF.Exp, scale=0.125),
                     reads=bks, writes=[ptB])

            def stageB(p):
                i, kt, n_ = p["i"], p["kt"], p["n"]
                par = i % 2
                pt, ptB = PtD[n_ % NP2], PtDB[n_ % NP2]
                first, last = p["first"], p["last"]
                for kv in range(2):
                    nb_ = (4 + kv + 2 * par) if moba else (6 + kv)
                    vl = Vd[:, kt, 0, 64:192] if kv == 0 else Vd[:, kt, 1, 0:128]
                    P.op("pe", lambda e, nb_=nb_, vl=vl, kv=kv: e.matmul(PS[nb_][:, 0:384], lhsT=vl, rhs=pt[:, kv, :], start=first, stop=last),
                         reads=[ptB, VdB[kt]], writes=[PSB[nb_]])
                return last

            def finalize(p):
                i = p["i"]
                par = i % 2
                r_, o_ = rr1, oo1
                nbs = [(4 + kv + 2 * par) if moba else (6 + kv) for kv in range(2)]
                for kv in range(2):
                    nb_ = nbs[kv]
                    H = slice(kv * 64, kv * 64 + 64)
                    Hd = slice((1 - kv) * 64, (1 - kv) * 64 + 64)
                    if moba:
                        P.op("dve", lambda e, H=H, Hd=Hd, nb_=nb_: e.reciprocal(out=r_[H], in_=PS[nb_][Hd, 0:384]), reads=[PSB[nb_]], writes=[rrB])
                    else:
                        P.op("dve", lambda e, kv=kv, nb_=nb_: e.tensor_copy(out=nd[kv], in_=PS[nb_][:, 0:384]), reads=[PSB[nb_]], writes=[ndB[kv]])
                        P.op("dve", lambda e, H=H, Hd=Hd, kv=kv: e.tensor_tensor(
                            out=r_[H].rearrange("p (a b) -> p a b", a=3), in0=nd[kv][Hd].rearrange("p (a b) -> p a b", a=3),
                            in1=es[Hd, l, kv * 3:kv * 3 + 3].unsqueeze(2).to_broadcast([64, 3, 128]), op=ALU.add),
                            reads=[ndB[kv], cB], writes=[rrB])
                if not moba:
                    P.op("act", lambda e: e.activation(out=r_, in_=r_, func=AF.Ln), reads=[rrB], writes=[rrB])
                    P.op("act", lambda e: e.activation(out=r_, in_=r_, func=AF.Exp, scale=-1.0), reads=[rrB], writes=[rrB])
                    return
                finalize2(p)

            def finalize2(p):
                i = p["i"]
                par = i % 2
                r_, o_ = rr1, oo1
                nbs = [(4 + kv + 2 * par) if moba else (6 + kv) for kv in range(2)]
                for kv in range(2):
                    nb_ = nbs[kv]
                    H = slice(kv * 64, kv * 64 + 64)
                    if moba:
                        P.op("dve", lambda e, H=H, nb_=nb_: e.tensor_tensor(out=o_[H], in0=PS[nb_][H, 0:384], in1=r_[H], op=ALU.mult),
                             reads=[PSB[nb_], rrB], writes=[ooB])
                    else:
                        P.op("dve", lambda e, H=H, kv=kv: e.tensor_tensor(out=o_[H], in0=nd[kv][H], in1=r_[H], op=ALU.mult),
                             reads=[ndB[kv], rrB], writes=[ooB])
                dst = yT[:, ych:ych + 3, i * 128:(i + 1) * 128]
                P.op("pool", lambda e, dst=dst: e.tensor_tensor(out=dst, in0=o_.rearrange("p (a b) -> p a b", a=3), in1=dst, op=ALU.mult),
                     reads=[ooB], writes=[yTB[(ych + c, i // 4)] for c in range(3)])

            LOOK = 3
            DEL = 2 if moba else 0
            fin = {}
            fin2 = {}
            trig = None
            if moba:
                trig = min(p["n"] for p in pairs if p["i"] == 5)
            for idx in range(len(pairs) + LOOK + DEL + 1):
                if moba and idx == trig:
                    prepass2()
                if idx < len(pairs):
                    stageA(pairs[idx])
                if LOOK <= idx < len(pairs) + LOOK:
                    if stageB(pairs[idx - LOOK]):
                        fin[idx + DEL] = pairs[idx - LOOK]
                    if not moba:
                        for f_ in range(3):
                            P.op("pe", lambda e, f_=f_: e.matmul(PS[4 + f_ % 2], lhsT=identb, rhs=qT[:, 0, 0:512], start=True, stop=True),
                                 reads=[cB], writes=[PSB[4 + f_ % 2]])
                if idx in fin2:
                    finalize2(fin2.pop(idx))
                if idx in fin:
                    p_ = fin.pop(idx)
                    finalize(p_)
                    if not moba:
                        fin2[idx + 2] = p_
            for idx in sorted(fin2):
                finalize2(fin2[idx])
            assert not fin
            nm = "ymoba" if moba else "yswa"
            tap(nm, yT[:, ych:ych + 3, :], [yTB[(c, n)] for c in range(ych, ych + 3) for n in range(4)])

        def outproj_phase(l, s, last):
            wis = [get_w(), get_w(prefetch=False)]
            for i in range(NT):
                for hfc in range(2):
                    bk = (i * 2 + hfc) % 2
                    wi = wis[hfc]
                    for k in range(8):
                        P.op("pe", lambda e, k=k, bk=bk, i=i, wi=wi: e.matmul(PS[bk], lhsT=yT[:, k, i * 128:(i + 1) * 128], rhs=Wb[wi][:, k, :],
                                                                             start=(k == 0), stop=(k == 7)),
                             reads=[WbB[wi]] + [yTB[(c, i // 4)] for c in range(8)], writes=[PSB[bk]])
                    xs = X[:, i, hfc * 512:(hfc + 1) * 512]
                    P.op("dve", lambda e, xs=xs, bk=bk: e.tensor_tensor(out=xs, in0=PS[bk], in1=xs, op=ALU.add), reads=[PSB[bk]], writes=[XB[i]])
                if last:
                    P.dma("sp", out_d[s, i * 128:(i + 1) * 128, :], X[:, i, :], reads=[XB[i]], is_output=True)
            prefetch_w()

        for s in range(nseq):
            for i in range(NT):
                P.dma("sp", X[:, i, :], x_d[s, i * 128:(i + 1) * 128, :], writes=[XB[i]])
            for l in range(2):
                norm_phase(l)
                tap(f"hT{l}", hT, hTB)
                if stop == "norm":
                    break
                ssm_phase(l)
                if stop == "ssm":
                    break
                attn_phase(l, False)
                if stop in ("swa", "attn_proj", "attn_1", "attn_A", "attn_B"):
                    break
                attn_phase(l, True)
                if stop == "moba":
                    break
                outproj_phase(l, s, l == 1)
                if stop == "layer0":
                    break
            if stop is not None:
                break
        P.finish()
        P.emit()
    return nc, tap_out


_CACHE = {}


def kernel(**inputs):
    inp = {k: np.asarray(v) for k, v in inputs.items()}
    n_cores = 8
    nseq = inp["x"].shape[0] // n_cores
    if "nc" not in _CACHE:
        _CACHE["nc"] = build(nseq)[0]
    nc = _CACHE["nc"]
    shared = dict(_host_consts())
    shared.update(_host_params(inp))
    shared["w_in"] = np.ascontiguousarray(inp["w_in"], dtype=np.float32)
    shared["w_out"] = np.ascontiguousarray(inp["w_out"], dtype=np.float32)
    shared["gluw"] = np.ascontiguousarray(inp["ssm_glu_w"], dtype=np.float32)
    x = np.ascontiguousarray(inp["x"], dtype=np.float32)
    in_maps = []
    for c in range(n_cores):
        m = dict(shared)
        m["x"] = x[c * nseq:(c + 1) * nseq]
        in_maps.append(m)
    res = run_bass_kernel_spmd(nc, in_maps, core_ids=list(range(n_cores)))
    out = np.concatenate([r["out"] for r in res.results], axis=0)
    return out.astype(np.float32)
```

```python
from contextlib import ExitStack
import numpy as np
import ml_dtypes
import concourse.bass as bass
import concourse.mybir as mybir
from concourse.bass_utils import run_bass_kernel_spmd

F32 = mybir.dt.float32
BF16 = mybir.dt.bfloat16
ALU = mybir.AluOpType
AF = mybir.ActivationFunctionType
AX = mybir.AxisListType

L = 2048
D = 1024
NT = 16
EPS = 1e-6
NEGBIG = -30000.0
MAGIC = 12582912.0
TWO_PI = float(2 * np.pi)


class Buf:
    __slots__ = ("name", "w", "r", "dsem", "dcnt")

    def __init__(self, name):
        self.name = name
        self.w = []
        self.r = []
        self.dsem = None
        self.dcnt = 0


class Prog:
    ENGS = ("pe", "dve", "act", "pool", "sp")

    def __init__(self, nc, stack):
        self.nc = nc
        self.stack = stack
        self.E = {}
        for e in self.ENGS:
            self.E[e] = dict(ops=[], sem=stack.enter_context(nc.semaphore("s_" + e)), cnt=0, waited={})
        self.nbuf = 0
        self.out_tickets = []
        self.dbufs = []

    def buf(self, name=None):
        self.nbuf += 1
        return Buf(name or f"b{self.nbuf}")

    def bufs(self, n, name="b"):
        return [self.buf(f"{name}{i}") for i in range(n)]

    def sb(self, name, shape, dtype):
        return self.stack.enter_context(self.nc.sbuf_tensor(name, list(shape), dtype))

    def ps(self, name, shape, dtype):
        return self.stack.enter_context(self.nc.psum_tensor(name, list(shape), dtype))

    def _waits(self, eng, reads, writes, skip_sem=None):
        E = self.E[eng]
        deps = []
        for b in reads:
            deps += b.w
        for b in writes:
            deps += b.w
            deps += b.r
        waits = []
        for (sem, val, src) in deps:
            if eng == "pe" and src == "pe":
                continue
            if skip_sem is not None and sem is skip_sem:
                continue
            if E["waited"].get(sem.num, 0) < val:
                E["waited"][sem.num] = val
                waits.append((sem, val))
        return waits

    def op(self, eng, fn, reads=(), writes=()):
        E = self.E[eng]
        waits = self._waits(eng, reads, writes)
        E["cnt"] += 1
        t = (E["sem"], E["cnt"], eng)
        E["ops"].append((waits, fn, (E["sem"], 1)))
        for b in reads:
            b.r.append(t)
        for b in writes:
            b.w = [t]
            b.r = []
        return t

    def dma(self, eng, out, in_, reads=(), writes=(), is_output=False, **kw):
        E = self.E[eng]
        key = writes[0] if writes else reads[0]
        if key.dsem is None:
            key.dsem = self.stack.enter_context(self.nc.semaphore(f"d{len(self.dbufs)}_" + key.name))
            self.dbufs.append(key)
        waits = self._waits(eng, reads, writes, skip_sem=key.dsem)
        key.dcnt += 1
        t = (key.dsem, 16 * key.dcnt, "dma")
        E["ops"].append((waits, (lambda e: e.dma_start(out=out, in_=in_, **kw)), (key.dsem, 16)))
        for b in reads:
            b.r.append(t)
        for b in writes:
            b.w = [t]
            b.r = []
        if is_output:
            self.out_tickets.append(t)
        return t

    def barrier(self):
        ticks = [(self.E[e]["sem"], self.E[e]["cnt"]) for e in self.ENGS if self.E[e]["cnt"] > 0]
        ticks += [(b.dsem, 16 * b.dcnt) for b in self.dbufs]
        for e in self.ENGS:
            E = self.E[e]
            waits = []
            for (sem, val) in ticks:
                if sem is E["sem"]:
                    continue
                if E["waited"].get(sem.num, 0) < val:
                    E["waited"][sem.num] = val
                    waits.append((sem, val))
            if waits:
                E["ops"].append((waits, None, None))

    def finish(self):
        self.barrier()

    def emit(self):
        nc = self.nc
        P = self

        def run(name, e):
            for (waits, fn, inc) in P.E[name]["ops"]:
                for (sem, val) in waits:
                    e.wait_ge(sem, val)
                if fn is None:
                    continue
                ins = fn(e)
                ins.then_inc(inc[0], inc[1])

        with nc.Block() as block:
            @block.tensor
            def _(e):
                run("pe", e)

            @block.vector
            def _(e):
                run("dve", e)

            @block.scalar
            def _(e):
                run("act", e)

            @block.gpsimd
            def _(e):
                run("pool", e)

            @block.sync
            def _(e):
                run("sp", e)


def _host_consts():
    c = {}
    c["ident"] = np.eye(128, dtype=np.float32)
    bo = np.zeros((128, 128), np.float32)
    bo[:64, :64] = 1.0
    bo[64:, 64:] = 1.0
    c["bones"] = bo
    s_idx = np.arange(128) // 16
    c["mask8"] = (s_idx[:, None] <= s_idx[None, :]).astype(np.float32)
    c["cidx"] = np.tile(np.arange(256, dtype=np.float32)[None, :], (128, 1))
    ev = np.concatenate([-np.arange(8), 7 - np.arange(8), np.arange(9)]).astype(np.float32)
    c["evals"] = np.tile(ev[None, :], (128, 1))
    sg = np.ones((128, 2), np.float32)
    sg[:64, 0] = -1.0
    sg[64:, 1] = -1.0
    c["sgn"] = sg
    k = np.arange(128)[:, None]
    q = np.arange(128)[None, :]
    diag = np.where(k <= q, 0.0, NEGBIG).astype(np.float32)
    prev = np.where(k > q, 0.0, NEGBIG).astype(np.float32)
    c["amask"] = np.stack([np.tile(diag, (1, 3)), np.tile(prev, (1, 3))], axis=1)
    ka = np.zeros((8, 8, 128), np.float32)
    for n in range(8):
        ka[n, n, :] = NEGBIG
    c["kaug"] = ka
    bm = np.zeros((128, 16, 2, 8), np.float32)
    for i in range(16):
        qb = i // 2
        bm[:, i, :, qb:] = -1e30
    c["bmask"] = bm
    return c


def _host_params(inp):
    f = lambda a: np.ascontiguousarray(np.asarray(a, dtype=np.float32))
    o = {}
    o["normg"] = f(inp["norm_g"].reshape(2, 8, 128).transpose(2, 0, 1))
    lr = inp["ssm_lam_re"].transpose(2, 0, 1)
    li = inp["ssm_lam_im"].transpose(2, 0, 1)
    ld = np.broadcast_to(inp["ssm_log_dt"][None, :, :], (64, 2, 16))
    A = np.stack([lr, li, ld], axis=2)
    o["ssmA"] = f(np.concatenate([A, A], axis=0))
    br = inp["ssm_b_re"].transpose(2, 0, 1, 3).reshape(64, 2, 256)
    bi = inp["ssm_b_im"].transpose(2, 0, 1, 3).reshape(64, 2, 256)
    o["ssmB"] = f(np.stack([np.concatenate([br, bi], 0), np.concatenate([bi, br], 0)], axis=2))
    cr = inp["ssm_c_re"].transpose(3, 0, 1, 2).reshape(64, 2, 256)
    ci = inp["ssm_c_im"].transpose(3, 0, 1, 2).reshape(64, 2, 256)
    o["ssmC"] = f(np.stack([np.concatenate([cr, ci], 0), np.concatenate([ci, cr], 0)], axis=2))
    d = inp["ssm_d"].reshape(2, 16, 16)
    o["dcol"] = f(np.tile(d.transpose(2, 0, 1)[None], (8, 1, 1, 1)).reshape(128, 2, 16))
    o["glub"] = f(inp["ssm_glu_b"].reshape(2, 2, 128).transpose(2, 0, 1))
    hn = np.stack([inp["swa_q_norm"], inp["swa_k_norm"], inp["moba_q_norm"], inp["moba_k_norm"]], axis=1)
    hn = hn.transpose(2, 0, 1)
    o["hn"] = f(np.concatenate([hn, hn], 0))
    o["sink"] = f(np.tile(inp["swa_sink"][None], (128, 1, 1)))
    return o


PARAM_SHAPES = {
    "normg": [128, 2, 8], "ssmA": [128, 2, 3, 16], "ssmB": [128, 2, 2, 256], "ssmC": [128, 2, 2, 256],
    "dcol": [128, 2, 16], "glub": [128, 2, 2], "hn": [128, 2, 4], "sink": [128, 2, 6],
}
CONST_SHAPES = {
    "ident": [128, 128], "bones": [128, 128], "mask8": [128, 128], "cidx": [128, 256], "evals": [128, 25],
    "sgn": [128, 2], "amask": [128, 2, 384], "kaug": [8, 8, 128], "bmask": [128, 16, 2, 8],
}


def _prod(s):
    r = 1
    for v in s:
        r *= v
    return r


def build(nseq, taps=None, stop=None):
    nc = bass.Bass("TRN2", target_bir_lowering=False)
    dram = {}
    x_d = nc.dram_tensor("x", [nseq, L, D], F32, kind="ExternalInput").ap()
    out_d = nc.dram_tensor("out", [nseq, L, D], F32, kind="ExternalOutput").ap()
    w_in_d = nc.dram_tensor("w_in", [2, D, 2560], F32, kind="ExternalInput").ap()
    w_out_d = nc.dram_tensor("w_out", [2, D, D], F32, kind="ExternalInput").ap()
    gluw_d = nc.dram_tensor("gluw", [2, 256, 256], F32, kind="ExternalInput").ap()
    for k, shp in list(PARAM_SHAPES.items()) + list(CONST_SHAPES.items()):
        dram[k] = nc.dram_tensor(k, shp, F32, kind="ExternalInput").ap()
    TB_d = nc.dram_tensor("TBs", [2, 128, 16 * 5 * 128], BF16, kind="Internal").ap()
    ROT_d = nc.dram_tensor("ROTs", [2, 128, 16 * 2 * 256], F32, kind="Internal").ap()
    tap_out = {}

    with ExitStack() as st:
        P = Prog(nc, st)
        ARENA = 53200
        arena = P.sb("arena", [128, ARENA], F32)

        def carve(off, shape, dtype, parts=128):
            n = _prod(shape[1:])
            esz = 4 if dtype == F32 else 2
            assert off % 4 == 0
            nw = (n * esz + 3) // 4
            assert off // 4 + nw <= ARENA, (off, shape)
            a = arena[0:parts, off // 4: off // 4 + nw]
            if dtype != F32:
                a = a.bitcast(dtype)[:, 0:n]
            if len(shape) == 3:
                a = a.rearrange("p (a b) -> p a b", a=shape[1])
            elif len(shape) == 4:
                a = a.rearrange("p (a b c) -> p a b c", a=shape[1], b=shape[2])
            elif len(shape) == 5:
                a = a.rearrange("p (a b c d) -> p a b c d", a=shape[1], b=shape[2], c=shape[3])
            return a

        def tap(name, ap, b, dtype=None):
            if taps is None or name not in taps or name in tap_out:
                return
            dt = dtype or ap.dtype
            d_ = nc.dram_tensor("tap_" + name, list(ap.shape), dt, kind="ExternalOutput").ap()
            tb = P.buf("tap_" + name)
            P.dma("sp", d_, ap, reads=list(b) if isinstance(b, (list, tuple)) else [b], writes=[tb], is_output=True)
            tap_out[name] = "tap_" + name

        X = carve(0, [128, 16, 1024], F32)
        XB = P.bufs(16, "X")
        hT = carve(65536, [128, 8, 2048], BF16)
        hTB = P.bufs(4, "hT")
        yT = carve(98304, [128, 8, 2048], BF16)
        yTB = {(c, n): P.buf(f"yT{c}_{n}") for c in range(8) for n in range(4)}
        Wb = [carve(131072, [128, 8, 512], BF16), carve(139264, [128, 8, 512], BF16)]
        WbB = P.bufs(2, "Wb")
        po = [147456]

        def pers(shape, dtype, parts=128):
            n = _prod(shape[1:]) * (4 if dtype == F32 else 2)
            n = (n + 3) // 4 * 4
            a = carve(po[0], shape, dtype, parts)
            po[0] += n
            return a

        identf = pers([128, 128], F32)
        identb = pers([128, 128], BF16)
        bones = pers([128, 128], BF16)
        amask = pers([128, 2, 384], BF16)
        kaug = pers([128, 8, 128], BF16)
        bmask = pers([128, 16, 2, 8], F32)
        normg = pers([128, 2, 8, 1], F32)
        glub = pers([128, 2, 2], F32)
        hn = pers([128, 2, 4], F32)
        es = pers([128, 2, 6], F32)
        rho = pers([128, 2, 16], F32)
        sgn = pers([128, 2], F32)
        gluw = pers([128, 2, 2, 256], BF16)
        ssq = pers([128, 16], F32)
        srt = pers([128, 16], F32)
        rstd = pers([128, 16], F32)
        hb = [pers([128, 1024], BF16), pers([128, 1024], BF16)]
        hbB = P.bufs(2, "hb")
        ones_b = pers([128, 128], BF16)
        cB = P.buf("consts")
        ssqB, srtB, rstdB = P.buf("ssq"), P.buf("srt"), P.buf("rstd")
        T0 = po[0]
        assert T0 <= 160768, T0
        T0 = 160768

        PSbig = [P.ps(f"psb{i}", [128, 1024], F32)[:] for i in range(4)]
        PS = []
        for i_ in range(4):
            PS.append(PSbig[i_][:, 0:512])
            PS.append(PSbig[i_][:, 512:1024])
        PSB = P.bufs(8, "ps")
        REG = {}

        def pbuf(name):
            if name not in REG:
                REG[name] = P.buf(name)
            return REG[name]

        def pbufs(n, name):
            return [pbuf(f"{name}{i}") for i in range(n)]

        def group(prefix):
            return [b for k_, b in REG.items() if k_.startswith(prefix)]

        fscr = pers([128, 4], F32)
        assert po[0] <= 160768 - 16, po[0]

        def fence(dead, new):
            allb = []
            for b in list(dead) + list(new):
                if b not in allb:
                    allb.append(b)
            P.op("pool", lambda e: e.memset(fscr, 0.0), writes=allb)

        def ld(dst, src, eng="pool"):
            P.dma("pool", dst, src, writes=[cB])

        ld(identf, dram["ident"])
        ld(identb, dram["ident"], "pool")
        ld(bones, dram["bones"], "pool")
        ld(amask, dram["amask"], "pool")
        P.op("pool", lambda e: e.memset(kaug, 0.0), writes=[cB])
        ld(kaug[0:8], dram["kaug"], "pool")
        ld(bmask, dram["bmask"])
        ld(normg[:, :, :, 0], dram["normg"])
        ld(glub, dram["glub"])
        ld(hn, dram["hn"])
        ld(es, dram["sink"])
        ld(sgn, dram["sgn"])
        for l in range(2):
            ld(gluw[:, l], gluw_d[l].rearrange("(e p) f -> p e f", p=128), "pool")
        P.op("act", lambda e: e.activation(out=es, in_=es, func=AF.Exp), reads=[cB], writes=[cB])
        P.op("pool", lambda e: e.memset(ones_b, 1.0), writes=[cB])

        def prologue(l):
            o = [0]

            def tmp(shape, dtype=F32):
                n = _prod(shape[1:]) * (4 if dtype == F32 else 2)
                a = carve(o[0], shape, dtype)
                o[0] += (n + 3) // 4 * 4
                return a

            A = tmp([128, 3, 16])
            Bp = tmp([128, 2, 16, 16])
            Cp = tmp([128, 2, 16, 16])
            dcol = tmp([128, 16])
            mask8 = tmp([128, 128])
            cidx = tmp([128, 256])
            ev = tmp([128, 25])
            b0 = P.buf("pro_in")
            P.dma("sp", A, dram["ssmA"][:, l], writes=[b0])
            P.dma("sp", Bp, dram["ssmB"][:, l].rearrange("p a (g h) -> p a g h", g=16), writes=[b0])
            P.dma("sp", Cp, dram["ssmC"][:, l].rearrange("p a (g h) -> p a g h", g=16), writes=[b0])
            P.dma("sp", dcol, dram["dcol"][:, l], writes=[b0])
            P.dma("sp", mask8, dram["mask8"], writes=[b0])
            P.dma("sp", cidx, dram["cidx"], writes=[b0])
            P.dma("sp", ev, dram["evals"], writes=[b0])
            pb = P.buf("pro")

            def V(fn, eng="dve"):
                P.op(eng, fn, reads=[b0, cB], writes=[pb])

            dt = tmp([128, 16])
            a_ = tmp([128, 16, 1])
            b_ = tmp([128, 16, 1])
            V(lambda e: e.activation(out=dt, in_=A[:, 2], func=AF.Exp), "act")
            V(lambda e: e.tensor_tensor(out=a_[:, :, 0], in0=A[:, 0], in1=dt, op=ALU.mult))
            V(lambda e: e.tensor_tensor(out=b_[:, :, 0], in0=A[:, 1], in1=dt, op=ALU.mult))
            NE = 25
            EA = tmp([128, 16, NE])
            ANG = tmp([128, 16, NE])
            KT = tmp([128, 16, NE])
            bc_e = lambda t: t.to_broadcast([128, 16, NE])
            evb = ev.unsqueeze(1).to_broadcast([128, 16, NE])
            V(lambda e: e.tensor_tensor(out=EA, in0=bc_e(a_), in1=evb, op=ALU.mult))
            V(lambda e: e.tensor_tensor(out=ANG, in0=bc_e(b_), in1=evb, op=ALU.mult))
            mag = tmp([128, 16, NE])
            V(lambda e: e.activation(out=mag, in_=EA, func=AF.Exp), "act")

            def sincos(ang, kt, sin_out, cos_out, shape):
                V(lambda e: e.tensor_scalar(out=kt, in0=ang, scalar1=1.0 / TWO_PI, scalar2=MAGIC, op0=ALU.mult, op1=ALU.add))
                V(lambda e: e.tensor_scalar(out=kt, in0=kt, scalar1=MAGIC, scalar2=None, op0=ALU.subtract))
                V(lambda e: e.scalar_tensor_tensor(out=kt, in0=kt, scalar=-TWO_PI, in1=ang, op0=ALU.mult, op1=ALU.add))
                V(lambda e: e.activation(out=sin_out, in_=kt, func=AF.Sin), "act")
                V(lambda e: e.tensor_scalar(out=ang, in0=ang, scalar1=float(np.pi / 2), scalar2=None, op0=ALU.add))
                V(lambda e: e.tensor_scalar(out=kt, in0=ang, scalar1=1.0 / TWO_PI, scalar2=MAGIC, op0=ALU.mult, op1=ALU.add))
                V(lambda e: e.tensor_scalar(out=kt, in0=kt, scalar1=MAGIC, scalar2=None, op0=ALU.subtract))
                V(lambda e: e.scalar_tensor_tensor(out=kt, in0=kt, scalar=-TWO_PI, in1=ang, op0=ALU.mult, op1=ALU.add))
                V(lambda e: e.activation(out=cos_out, in_=kt, func=AF.Sin), "act")

            Er = tmp([128, 16, NE])
            Ei = tmp([128, 16, NE])
            sincos(ANG, KT, Ei, Er, None)
            V(lambda e: e.tensor_tensor(out=Er, in0=Er, in1=mag, op=ALU.mult))
            V(lambda e: e.tensor_tensor(out=Ei, in0=Ei, in1=mag, op=ALU.mult))
            Eis = tmp([128, 16, NE])
            Eisn = tmp([128, 16, NE])
            Ein = tmp([128, 16, NE])
            Ers2 = tmp([128, 16, NE])
            Ers0 = tmp([128, 16, NE])
            V(lambda e: e.tensor_scalar(out=Eis, in0=Ei, scalar1=sgn[:, 0:1], scalar2=None, op0=ALU.mult))
            V(lambda e: e.tensor_scalar(out=Eisn, in0=Ei, scalar1=sgn[:, 1:2], scalar2=None, op0=ALU.mult))
            V(lambda e: e.tensor_scalar(out=Ein, in0=Ei, scalar1=-1.0, scalar2=None, op0=ALU.mult))
            V(lambda e: e.tensor_scalar(out=Ers2, in0=Er, scalar1=sgn[:, 1:2], scalar2=None, op0=ALU.mult))
            V(lambda e: e.tensor_scalar(out=Ers0, in0=Er, scalar1=sgn[:, 0:1], scalar2=None, op0=ALU.mult))
            nr = tmp([128, 16])
            t1 = tmp([128, 16])
            t2 = tmp([128, 16])
            den = tmp([128, 16])
            cr = tmp([128, 16, 1])
            ci = tmp([128, 16, 1])
            cis = tmp([128, 16, 1])
            cisn = tmp([128, 16, 1])
            V(lambda e: e.tensor_scalar(out=nr, in0=Er[:, :, 17], scalar1=-1.0, scalar2=None, op0=ALU.add))
            V(lambda e: e.tensor_tensor(out=t1, in0=A[:, 0], in1=A[:, 0], op=ALU.mult))
            V(lambda e: e.tensor_tensor(out=t2, in0=A[:, 1], in1=A[:, 1], op=ALU.mult))
            V(lambda e: e.tensor_tensor(out=den, in0=t1, in1=t2, op=ALU.add))
            V(lambda e: e.reciprocal(out=den, in_=den))
            V(lambda e: e.tensor_tensor(out=t1, in0=nr, in1=A[:, 0], op=ALU.mult))
            V(lambda e: e.tensor_tensor(out=t2, in0=Ei[:, :, 17], in1=A[:, 1], op=ALU.mult))
            V(lambda e: e.tensor_tensor(out=t1, in0=t1, in1=t2, op=ALU.add))
            V(lambda e: e.tensor_tensor(out=cr[:, :, 0], in0=t1, in1=den, op=ALU.mult))
            V(lambda e: e.tensor_tensor(out=t1, in0=Ei[:, :, 17], in1=A[:, 0], op=ALU.mult))
            V(lambda e: e.tensor_tensor(out=t2, in0=nr, in1=A[:, 1], op=ALU.mult))
            V(lambda e: e.tensor_tensor(out=t1, in0=t1, in1=t2, op=ALU.subtract))
            V(lambda e: e.tensor_tensor(out=ci[:, :, 0], in0=t1, in1=den, op=ALU.mult))
            V(lambda e: e.tensor_scalar(out=cis[:, :, 0], in0=ci[:, :, 0], scalar1=sgn[:, 0:1], scalar2=None, op0=ALU.mult))
            V(lambda e: e.tensor_scalar(out=cisn[:, :, 0], in0=ci[:, :, 0], scalar1=sgn[:, 1:2], scalar2=None, op0=ALU.mult))
            Bb = tmp([128, 16, 16])
            Bbw = tmp([128, 16, 16])
            tt = tmp([128, 16, 16])
            bc_h = lambda t: t.to_broadcast([128, 16, 16])
            V(lambda e: e.tensor_tensor(out=Bb, in0=bc_h(cr), in1=Bp[:, 0], op=ALU.mult))
            V(lambda e: e.tensor_tensor(out=tt, in0=bc_h(cis), in1=Bp[:, 1], op=ALU.mult))
            V(lambda e: e.tensor_tensor(out=Bb, in0=Bb, in1=tt, op=ALU.add))
            V(lambda e: e.tensor_tensor(out=Bbw, in0=bc_h(cr), in1=Bp[:, 1], op=ALU.mult))
            V(lambda e: e.tensor_tensor(out=tt, in0=bc_h(cisn), in1=Bp[:, 0], op=ALU.mult))
            V(lambda e: e.tensor_tensor(out=Bbw, in0=Bbw, in1=tt, op=ALU.add))
            TT = tmp([128, 16, 8, 16])

            def table(E1, e0, B1, E2, B2):
                out = tmp([128, 16, 8, 16])
                e1 = E1[:, :, e0:e0 + 8].unsqueeze(3).to_broadcast([128, 16, 8, 16])
                e2 = E2[:, :, e0:e0 + 8].unsqueeze(3).to_broadcast([128, 16, 8, 16])
                b1 = B1.unsqueeze(2).to_broadcast([128, 16, 8, 16])
                b2 = B2.unsqueeze(2).to_broadcast([128, 16, 8, 16])
                V(lambda e: e.tensor_tensor(out=out, in0=e1, in1=b1, op=ALU.mult))
                V(lambda e: e.tensor_tensor(out=TT, in0=e2, in1=b2, op=ALU.mult))
                V(lambda e: e.tensor_tensor(out=out, in0=out, in1=TT, op=ALU.add))
                return out

            Bneg = table(Er, 0, Bb, Eis, Bbw)
            B7 = table(Er, 8, Bb, Eis, Bbw)
            B7x = table(Er, 8, Bbw, Eisn, Bb)
            Cpos = table(Ers2, 16, Cp[:, 0], Ein, Cp[:, 1])
            Cx = table(Ers2, 17, Cp[:, 0], Ein, Cp[:, 1])
            Cxx = table(Ers0, 17, Cp[:, 1], Ein, Cp[:, 0])
            TBs = tmp([128, 16, 5, 128], BF16)
            fl = lambda t, g: t[:, g].rearrange("p s h -> p (s h)")
            V(lambda e: e.tensor_copy(out=TBs[:, :, 2, :], in_=Cx.rearrange("p g s h -> p g (s h)")))
            V(lambda e: e.tensor_copy(out=TBs[:, :, 3, :], in_=Cxx.rearrange("p g s h -> p g (s h)")))
            mtmp = tmp([128, 128])
            for g in range(16):
                bk = g % 2
                P.op("pe", lambda e, g=g, bk=bk: e.matmul(PS[bk][:, 0:128], lhsT=fl(Bneg, g), rhs=fl(Cpos, g), start=True, stop=True),
                     reads=[pb], writes=[PSB[bk]])
                P.op("pe", lambda e, g=g, bk=bk: e.transpose(out=PS[bk][:, 128:256], in_=fl(B7, g), identity=identf),
                     reads=[pb, cB], writes=[PSB[bk]])
                P.op("pe", lambda e, g=g, bk=bk: e.transpose(out=PS[bk][:, 256:384], in_=fl(B7x, g), identity=identf),
                     reads=[pb, cB], writes=[PSB[bk]])
                tb2 = P.buf()
                P.op("dve", lambda e, bk=bk: e.tensor_tensor(out=mtmp, in0=PS[bk][:, 0:128], in1=mask8, op=ALU.mult),
                     reads=[PSB[bk], b0, pb], writes=[pb])
                P.op("dve", lambda e, g=g: e.scalar_tensor_tensor(out=TBs[:, g, 4, :], in0=identf, scalar=dcol[:, g:g + 1], in1=mtmp,
                                                                    op0=ALU.mult, op1=ALU.add), reads=[pb, cB, b0], writes=[pb])
                P.op("act", lambda e, g=g, bk=bk: e.activation(out=TBs[:, g, 0:2, :], in_=PS[bk][:, 128:384].rearrange("p (a b) -> p a b", a=2), func=AF.Copy),
                     reads=[PSB[bk], pb], writes=[pb])
            P.dma("sp", TB_d[l], TBs.rearrange("p g a b -> p (g a b)"), reads=[pb], writes=[TBdB[l]])
            phi = tmp([128, 16, 1])
            kt1 = tmp([128, 16, 1])
            V(lambda e: e.tensor_scalar(out=phi, in0=b_, scalar1=8.0, scalar2=None, op0=ALU.mult))
            V(lambda e: e.tensor_scalar(out=kt1, in0=phi, scalar1=1.0 / TWO_PI, scalar2=MAGIC, op0=ALU.mult, op1=ALU.add))
            V(lambda e: e.tensor_scalar(out=kt1, in0=kt1, scalar1=MAGIC, scalar2=None, op0=ALU.subtract))
            V(lambda e: e.scalar_tensor_tensor(out=phi, in0=kt1, scalar=-TWO_PI, in1=phi, op0=ALU.mult, op1=ALU.add))
            o[0] = 0
            keep = carve(190000, [128, 2, 16], F32)
            V(lambda e: e.tensor_copy(out=keep[:, 0], in_=phi[:, :, 0]))
            V(lambda e: e.activation(out=rho[:, l], in_=a_[:, :, 0], func=AF.Exp, scale=8.0), "act")
            cidx2 = carve(192000, [128, 256], F32)
            V(lambda e: e.tensor_copy(out=cidx2, in_=cidx))
            ANG2 = tmp([128, 16, 256])
            KT2 = tmp([128, 16, 256])
            ROTs = tmp([128, 16, 2, 256])
            V(lambda e: e.tensor_tensor(out=ANG2, in0=keep[:, 0].unsqueeze(2).to_broadcast([128, 16, 256]),
                                        in1=cidx2.unsqueeze(1).to_broadcast([128, 16, 256]), op=ALU.mult))
            sincos(ANG2, KT2, ROTs[:, :, 1, :], ROTs[:, :, 0, :], None)
            V(lambda e: e.tensor_scalar(out=ROTs[64:128, :, 1, :], in0=ROTs[64:128, :, 1, :], scalar1=-1.0, scalar2=None, op0=ALU.mult))
            P.dma("sp", ROT_d[l], ROTs.rearrange("p g a b -> p (g a b)"), reads=[pb], writes=[ROTdB[l]])

        TBdB = P.bufs(2, "TBd")
        ROTdB = P.bufs(2, "ROTd")
        for l in range(2):
            prologue(l)
            P.barrier()
        if stop == "prologue":
            tap("rho", rho, cB)
            P.finish()
            P.emit()
            return nc, tap_out

        w_in_v = [w_in_d[l].rearrange("(k p) c -> p k c", p=128) for l in range(2)]
        w_out_v = [w_out_d[l].rearrange("(k p) c -> p k c", p=128) for l in range(2)]
        hT5 = hT.rearrange("p k (hf c s) -> p k hf c s", hf=2, s=8)
        wsel = [0]

        def attn_pieces(base, which):
            pcs = []
            off0 = 0 if which == "A" else 640
            for s_ in range(3):
                pcs.append((base + off0 + s_ * 64, 64, s_ * 128))
                pcs.append((base + off0 + (3 + s_) * 64, 64, s_ * 128 + 64))
            pcs.append((base + (384 if which == "A" else 512), 128, 384))
            return pcs

        def mk_in(l, pieces):
            def issue(wi):
                for (c0, n, off) in pieces:
                    P.dma("pool", Wb[wi][:, :, off:off + n], w_in_v[l][:, :, c0:c0 + n], writes=[WbB[wi]])
            return issue

        def mk_out(l, hfc):
            def issue(wi):
                c0 = hfc * 512
                P.dma("pool", Wb[wi][:, 0:2, :], w_out_v[l][:, 0:2, c0:c0 + 512], writes=[WbB[wi]])
                for br in range(2):
                    for s_ in range(3):
                        k = 2 + br * 3 + s_
                        ra = 256 + br * 384 + s_ * 64
                        rb = 256 + br * 384 + (3 + s_) * 64
                        P.dma("pool", Wb[wi][0:64, k, :], w_out_d[l][ra:ra + 64, c0:c0 + 512], writes=[WbB[wi]])
                        P.dma("pool", Wb[wi][64:128, k, :], w_out_d[l][rb:rb + 64, c0:c0 + 512], writes=[WbB[wi]])
            return issue

        WQ = []
        for s__ in range(nseq):
            for l__ in range(2):
                WQ.append(mk_in(l__, [(0, 512, 0)]))
                for base__ in (512, 1536):
                    WQ.append(mk_in(l__, attn_pieces(base__, "A")))
                    WQ.append(mk_in(l__, attn_pieces(base__, "B")))
                WQ.append(mk_out(l__, 0))
                WQ.append(mk_out(l__, 1))
        wq_pos = [0]
        wq_issued = [0]

        def get_w(prefetch=True):
            n = wq_pos[0]
            wq_pos[0] += 1
            for m in ((n, n + 1) if prefetch else (n,)):
                if m < len(WQ) and m >= wq_issued[0]:
                    WQ[m](m % 2)
                    wq_issued[0] = m + 1
            return n % 2

        def prefetch_w(ahead=0):
            m = wq_pos[0] + ahead
            if m < len(WQ) and m == wq_issued[0]:
                WQ[m](m % 2)
                wq_issued[0] = m + 1

        pbank = [0]

        def proj_fm(wi, col_off, n, bank_pool=(0, 1)):
            bk = bank_pool[pbank[0] % len(bank_pool)]
            pbank[0] += 1
            for k in range(8):
                P.op("pe", lambda e, k=k, bk=bk: e.matmul(PS[bk][:, :], lhsT=Wb[wi][:, k, col_off:col_off + 128],
                                                          rhs=hT[:, k, n * 512:(n + 1) * 512], start=(k == 0), stop=(k == 7)),
                     reads=[WbB[wi], hTB[n]], writes=[PSB[bk]])
            return bk

        def norm_phase(l):
            junk = hT[:, 7, 0:1024]
            for i in range(NT):
                P.op("act", lambda e, i=i: e.activation(out=junk, in_=X[:, i, :], func=AF.Square, accum_out=ssq[:, i:i + 1]),
                     reads=[XB[i]], writes=[hTB[0], hTB[1], ssqB])
            P.op("act", lambda e: e.activation(out=srt, in_=ssq, func=AF.Sqrt, scale=1.0 / D, bias=epsb[:, 0:1]), reads=[ssqB, cB], writes=[srtB])
            P.op("dve", lambda e: e.reciprocal(out=rstd, in_=srt), reads=[srtB], writes=[rstdB])
            for i in range(NT):
                j = i % 2
                P.op("act", lambda e, i=i, j=j: e.activation(out=hb[j], in_=X[:, i, :], func=AF.Copy, scale=rstd[:, i:i + 1]),
                     reads=[XB[i], rstdB], writes=[hbB[j]])
                tp = PS[6 + j].bitcast(BF16).rearrange("p (k t) -> p k t", k=8)
                for k in range(8):
                    P.op("pe", lambda e, j=j, k=k, tp=tp: e.transpose(out=tp[:, k, :], in_=hb[j][:, k * 128:(k + 1) * 128], identity=identb),
                         reads=[hbB[j], cB], writes=[PSB[6 + j]])
                P.op("dve", lambda e, i=i, tp=tp: e.tensor_tensor(out=hT[:, :, i * 128:(i + 1) * 128], in0=tp,
                                                                  in1=normg[:, l].to_broadcast([128, 8, 128]), op=ALU.mult),
                     reads=[PSB[6 + j], cB], writes=[hTB[i // 4]])

        epsb = pers([128, 1], F32) if False else carve(T0 - 16, [128, 1], F32)
        P.op("pool", lambda e: e.memset(epsb, EPS), writes=[cB])

        def ssm_phase(l):
            o = [T0]

            def tmp(shape, dtype=F32):
                n = _prod(shape[1:]) * (4 if dtype == F32 else 2)
                a = carve(o[0], shape, dtype)
                o[0] += (n + 3) // 4 * 4
                return a

            Utok = tmp([128, 2, 8, 256], BF16)
            Utok5 = Utok.rearrange("p hf (g s) (a h) -> p hf (g a) s h", g=8, a=2) if False else Utok.rearrange("p hf s x -> p hf (s x)").rearrange("p hf (g s h) -> p hf g s h", g=16, s=8)
            UtokB = pbuf("ssm.Utok")
            Ug = tmp([128, 16, 256], BF16)
            UgB = pbufs(16, "ssm.Ug")
            Yg = tmp([128, 16, 256], BF16)
            YgB = pbufs(16, "ssm.Yg")
            TBb = [tmp([128, 2, 5, 128], BF16) for _ in range(2)]
            TBbB = pbufs(3, "ssm.b.TBb")
            ROTb = [tmp([128, 2, 2, 256], F32) for _ in range(2)]
            ROTbB = pbufs(3, "ssm.b.ROTb")
            TBb.append(carve(T0, [128, 2, 5, 128], BF16))
            ROTb.append(carve(T0 + 2560, [128, 2, 2, 256], F32))
            T1 = [tmp([128, 2, 256], F32) for _ in range(2)]
            T2 = [tmp([128, 2, 256], F32) for _ in range(2)]
            T1B, T2B = pbufs(2, "ssm.b.T1"), pbufs(2, "ssm.b.T2")
            Pb = [tmp([128, 2, 258], BF16) for _ in range(2)]
            Qb = [tmp([128, 2, 258], BF16) for _ in range(2)]
            PQB = pbufs(2, "ssm.b.PQ")
            ypB = pbufs(2, "ssm.e.yp")
            w1Bs = pbufs(2, "ssm.e.w1")
            yglB = pbufs(2, "ssm.e.ygl")
            sg2Bs = pbufs(2, "ssm.e.sg2")
            fence(group("att."), [b for b in group("ssm.") if not b.name.startswith("ssm.e.")])
            assert o[0] <= ARENA * 4, o[0]
            wi = get_w()
            for c in range(2):
                for n in range(4):
                    bk = proj_fm(wi, 256 + c * 128, n)
                    P.op("act", lambda e, c=c, n=n, bk=bk: e.activation(out=yT[:, c, n * 512:(n + 1) * 512], in_=PS[bk], func=AF.Silu),
                         reads=[PSB[bk]], writes=[yTB[(c, n)]])
            for hf in range(2):
                for s in range(8):
                    bk = (hf * 8 + s) % 2 + 2
                    for k in range(8):
                        P.op("pe", lambda e, k=k, bk=bk, hf=hf, s=s: e.matmul(PS[bk][:, 0:256], lhsT=hT5[:, k, hf, :, s], rhs=Wb[wi][:, k, 0:256],
                                                                             start=(k == 0), stop=(k == 7)),
                             reads=[WbB[wi], hTB[2 * hf], hTB[2 * hf + 1]], writes=[PSB[bk]])
                    P.op("act", lambda e, bk=bk, hf=hf, s=s: e.activation(out=Utok5[:, hf, :, s, :], in_=PS[bk][:, 0:256].rearrange("p (g h) -> p g h", g=16), func=AF.Copy),
                         reads=[PSB[bk]], writes=[UtokB])
            prefetch_w(1)
            for j_ in range(2):
                P.op("pool", lambda e, j_=j_: e.memset(Pb[j_], 0.0), writes=[PQB[j_]])
                P.op("pool", lambda e, j_=j_: e.memset(Qb[j_], 0.0), writes=[PQB[j_]])
            for hf in range(2):
                for g8 in range(2):
                    bk = 6 + (hf * 2 + g8) % 2
                    tp = PS[bk].bitcast(BF16).rearrange("p (g c) -> p g c", g=8)
                    for gi in range(8):
                        g = g8 * 8 + gi
                        P.op("pe", lambda e, tp=tp, gi=gi, g=g, hf=hf: e.transpose(out=tp[:, gi, :], in_=Utok5[:, hf, g].rearrange("p s h -> p (s h)"), identity=identb),
                             reads=[UtokB, cB], writes=[PSB[bk]])
                    P.op("dve", lambda e, tp=tp, g8=g8, hf=hf: e.tensor_copy(out=Ug[:, g8 * 8:(g8 + 1) * 8, hf * 128:(hf + 1) * 128], in_=tp),
                         reads=[PSB[bk]], writes=UgB[g8 * 8:(g8 + 1) * 8])
            tap("Ug", Ug, UgB)
            TBv = TB_d[l].rearrange("p (g a b) -> p g a b", g=16, a=5)
            ROTv = ROT_d[l].rearrange("p (g a b) -> p g a b", g=16, a=2)
            def ld_tab(bt):
                j3 = bt % 3
                g0 = bt * 2
                P.dma("sp", TBb[j3], TBv[:, g0:g0 + 2], reads=[TBdB[l]], writes=[TBbB[j3]])
                P.dma("sp", ROTb[j3], ROTv[:, g0:g0 + 2], reads=[ROTdB[l]], writes=[ROTbB[j3]])

            def s1(bt):
                j = bt % 2
                j3 = bt % 3
                g0 = bt * 2
                b0_, b1_ = 2 * j, 2 * j + 1
                for gi in range(2):
                    g = g0 + gi
                    P.op("pe", lambda e, gi=gi, g=g: e.matmul(PS[b0_][:, gi * 256:(gi + 1) * 256], lhsT=TBb[j3][:, gi, 0, :], rhs=Ug[:, g, :], start=True, stop=True),
                         reads=[TBbB[j3], UgB[g]], writes=[PSB[b0_]])
                    P.op("pe", lambda e, gi=gi, g=g: e.matmul(PS[b1_][:, gi * 256:(gi + 1) * 256], lhsT=TBb[j3][:, gi, 1, :], rhs=Ug[:, g, :], start=True, stop=True),
                         reads=[TBbB[j3], UgB[g]], writes=[PSB[b1_]])

            def s2(bt):
                j = bt % 2
                j3 = bt % 3
                g0 = bt * 2
                b0_, b1_ = 2 * j, 2 * j + 1
                ps0 = PS[b0_].rearrange("p (g c) -> p g c", g=2)
                ps1 = PS[b1_].rearrange("p (g c) -> p g c", g=2)
                P.op("dve", lambda e: e.tensor_tensor(out=T1[j], in0=ps0, in1=ROTb[j3][:, :, 0, :], op=ALU.mult),
                     reads=[PSB[b0_], ROTbB[j3]], writes=[T1B[j]])
                P.op("dve", lambda e: e.tensor_tensor(out=T2[j], in0=ps1, in1=ROTb[j3][:, :, 1, :], op=ALU.mult),
                     reads=[PSB[b1_], ROTbB[j3]], writes=[T2B[j]])
                P.op("dve", lambda e: e.tensor_tensor(out=T1[j], in0=T1[j], in1=T2[j], op=ALU.add), reads=[T1B[j], T2B[j]], writes=[T1B[j]])
                for gi in range(2):
                    g = g0 + gi
                    P.op("dve", lambda e, gi=gi, g=g: e.tensor_tensor_scan(out=T2[j][:, gi, :], data0=rho[:, l, g:g + 1].to_broadcast([128, 256]),
                                                                          data1=T1[j][:, gi, :], initial=0.0, op0=ALU.mult, op1=ALU.add),
                         reads=[T1B[j], cB], writes=[T2B[j]])
                P.op("pool", lambda e: e.tensor_tensor(out=Pb[j][:, :, 1:257], in0=T2[j], in1=ROTb[j3][:, :, 0, :], op=ALU.mult),
                     reads=[T2B[j], ROTbB[j3]], writes=[PQB[j]])
                P.op("pool", lambda e: e.tensor_tensor(out=Qb[j][:, :, 1:257], in0=T2[j], in1=ROTb[j3][:, :, 1, :], op=ALU.mult),
                     reads=[T2B[j], ROTbB[j3]], writes=[PQB[j]])

            def s3(bt):
                j = bt % 2
                j3 = bt % 3
                g0 = bt * 2
                yb = 4 + bt % 2
                for gi in range(2):
                    g = g0 + gi
                    yo = PS[yb][:, gi * 256:(gi + 1) * 256]
                    P.op("pe", lambda e, gi=gi, g=g, yo=yo: e.matmul(yo, lhsT=TBb[j3][:, gi, 4, :], rhs=Ug[:, g, :], start=True, stop=False),
                         reads=[TBbB[j3], UgB[g]], writes=[PSB[yb]])
                    P.op("pe", lambda e, gi=gi, yo=yo: e.matmul(yo, lhsT=TBb[j3][:, gi, 2, :], rhs=Pb[j][:, gi, 0:256], start=False, stop=False),
                         reads=[TBbB[j3], PQB[j]], writes=[PSB[yb]])
                    P.op("pe", lambda e, gi=gi, yo=yo: e.matmul(yo, lhsT=TBb[j3][:, gi, 3, :], rhs=Qb[j][:, gi, 0:256], start=False, stop=True),
                         reads=[TBbB[j3], PQB[j]], writes=[PSB[yb]])
                P.op("act", lambda e: e.activation(out=Yg[:, g0:g0 + 2, :], in_=PS[yb].rearrange("p (g c) -> p g c", g=2), func=AF.Copy),
                     reads=[PSB[yb]], writes=YgB[g0:g0 + 2])

            fence([UtokB], [TBbB[2], ROTbB[2]])
            ld_tab(0)
            ld_tab(1)
            ld_tab(2)
            s1(0)
            for bt in range(8):
                s2(bt)
                if bt + 1 < 8:
                    s1(bt + 1)
                s3(bt)
                if bt + 3 < 8:
                    ld_tab(bt + 3)
            tap("Yg", Yg, YgB)
            fence([TBbB[2], ROTbB[2]], [UtokB])
            Ytok = Utok
            for hf in range(2):
                for g8 in range(2):
                    bk = 6 + (hf * 2 + g8) % 2
                    tp = PS[bk].bitcast(BF16).rearrange("p (g t h) -> p g t h", g=8, t=8)
                    tpf = PS[bk].bitcast(BF16).rearrange("p (g c) -> p g c", g=8)
                    for gi in range(8):
                        g = g8 * 8 + gi
                        P.op("pe", lambda e, tpf=tpf, gi=gi, g=g, hf=hf: e.transpose(out=tpf[:, gi, :], in_=Yg[:, g, hf * 128:(hf + 1) * 128], identity=identb),
                             reads=[YgB[g], cB], writes=[PSB[bk]])
                    P.op("dve", lambda e, tp=tp, g8=g8, hf=hf: e.tensor_copy(
                        out=Ytok[:, hf, :, g8 * 128:(g8 + 1) * 128].rearrange("p t (g h) -> p g t h", g=8), in_=tp),
                        reads=[PSB[bk]], writes=[UtokB])
            o2 = [o[0] - 0]
            o[0] = T0 + 8192 + 8192 + 8192
            fence(group("ssm.b."), group("ssm.e."))
            yp = [tmp([128, 8, 128], BF16) for _ in range(2)]
            w1s = [tmp([128, 1024], F32) for _ in range(2)]
            ygl = [tmp([128, 1024], BF16) for _ in range(2)]
            sg2s = [tmp([128, 512], F32) for _ in range(2)]
            assert o[0] <= ARENA * 4
            gcnt = 0
            for hf in range(2):
                for ch in range(2):
                    bk = 6 + ch
                    w1, w1B = w1s[ch], w1Bs[ch]
                    tp = PS[bk].bitcast(BF16).rearrange("p (t c) -> p t c", t=8)
                    for t in range(8):
                        P.op("pe", lambda e, tp=tp, t=t, ch=ch, hf=hf: e.transpose(out=tp[:, t, :], in_=Ytok[:, hf, t, ch * 128:(ch + 1) * 128], identity=identb),
                             reads=[UtokB, cB], writes=[PSB[bk]])
                    P.op("act", lambda e, tp=tp, ch=ch: e.activation(out=yp[ch], in_=tp, func=AF.Copy), reads=[PSB[bk]], writes=[ypB[ch]])
                    ypf = yp[ch].rearrange("p t c -> p (t c)")
                    P.op("dve", lambda e, ypf=ypf, w1=w1: e.tensor_tensor(out=w1, in0=ypf, in1=ypf, op=ALU.mult), reads=[ypB[ch]], writes=[w1B])
                    P.op("dve", lambda e, w1=w1: e.tensor_scalar(out=w1, in0=w1, scalar1=0.044715, scalar2=1.0, op0=ALU.mult, op1=ALU.add), reads=[w1B], writes=[w1B])
                    P.op("dve", lambda e, ypf=ypf, w1=w1: e.tensor_tensor(out=w1, in0=w1, in1=ypf, op=ALU.mult), reads=[w1B, ypB[ch]], writes=[w1B])
                    P.op("act", lambda e, w1=w1: e.activation(out=w1, in_=w1, func=AF.Sigmoid, scale=1.5957691216057308), reads=[w1B], writes=[w1B])
                    P.op("dve", lambda e, ypf=ypf, ch=ch, w1=w1: e.tensor_tensor(out=ygl[ch], in0=w1, in1=ypf, op=ALU.mult), reads=[w1B, ypB[ch]], writes=[yglB[ch]])
                for f in range(2):
                    for hh in range(2):
                        bk = (f * 2 + hh) % 2
                        sg2, sg2B = sg2s[gcnt % 2], sg2Bs[gcnt % 2]
                        gcnt += 1
                        for ec in range(2):
                            P.op("pe", lambda e, ec=ec, f=f, hh=hh, bk=bk: e.matmul(PS[bk], lhsT=gluw[:, l, ec, f * 128:(f + 1) * 128],
                                                                                   rhs=ygl[ec][:, hh * 512:(hh + 1) * 512], start=(ec == 0), stop=(ec == 1)),
                                 reads=[yglB[0], yglB[1], cB], writes=[PSB[bk]])
                        P.op("act", lambda e, bk=bk, f=f, sg2=sg2: e.activation(out=sg2, in_=PS[bk], func=AF.Sigmoid, bias=glub[:, l, f:f + 1]),
                             reads=[PSB[bk], cB], writes=[sg2B])
                        P.op("dve", lambda e, f=f, hh=hh, sg2=sg2: e.tensor_tensor(out=sg2, in0=sg2, in1=ygl[f][:, hh * 512:(hh + 1) * 512], op=ALU.mult),
                             reads=[sg2B, yglB[f]], writes=[sg2B])
                        dst = yT[:, f, hf * 1024:(hf + 1) * 1024].rearrange("p (c t) -> p t c", t=8)[:, 4 * hh:4 * hh + 4, :]
                        P.op("pool", lambda e, dst=dst, sg2=sg2: e.tensor_tensor(out=dst, in0=sg2.rearrange("p (t c) -> p t c", t=4), in1=dst, op=ALU.mult),
                             reads=[sg2B], writes=[yTB[(f, 2 * hf)], yTB[(f, 2 * hf + 1)]])
            tap("yssm", yT[:, 0:2, :], [yTB[(c, n)] for c in range(2) for n in range(4)])

        def head_norm1(bk, tmpb):
            sqb, srtb, rinvb, sqB, srtB2, rinvB = tmpb
            P.op("act", lambda e: e.activation(out=sqb, in_=PS[bk], func=AF.Square), reads=[PSB[bk]], writes=[sqB])

        def head_norm2(bk, l, idx, dst, dstB, tmpb, b2):
            sqb, srtb, rinvb, sqB, srtB2, rinvB = tmpb
            P.op("pe", lambda e: e.matmul(PS[b2], lhsT=bones, rhs=sqb, start=True, stop=True), reads=[sqB, cB], writes=[PSB[b2]])
            P.op("act", lambda e: e.activation(out=rinvb, in_=PS[b2], func=AF.Ln, scale=1.0 / 64, bias=epsb[:, 0:1]), reads=[PSB[b2], cB], writes=[rinvB])
            P.op("act", lambda e: e.activation(out=rinvb, in_=rinvb, func=AF.Exp, scale=-0.5), reads=[rinvB], writes=[rinvB])
            if isinstance(dst, tuple):
                kTz, n = dst[2], dst[1]
                for hv in range(2):
                    H = slice(hv * 64, hv * 64 + 64)
                    P.op("dve", lambda e, H=H, hv=hv: e.scalar_tensor_tensor(out=kTz[H, hv, n * 512:(n + 1) * 512], in0=PS[bk][H], scalar=hn[H, l, idx:idx + 1], in1=rinvb[H],
                                                                         op0=ALU.mult, op1=ALU.mult),
                         reads=[PSB[bk], rinvB, cB], writes=[dstB])
                return
            P.op("dve", lambda e: e.scalar_tensor_tensor(out=dst, in0=PS[bk], scalar=hn[:, l, idx:idx + 1], in1=rinvb, op0=ALU.mult, op1=ALU.mult),
                 reads=[PSB[bk], rinvB, cB], writes=[dstB])

        def attn_phase(l, moba):
            base = 1536 if moba else 512
            ych = 5 if moba else 2
            o = [T0]

            def tmp(shape, dtype=F32, parts=128):
                n = _prod(shape[1:]) * (4 if dtype == F32 else 2)
                a = carve(o[0], shape, dtype, parts)
                o[0] += (n + 3) // 4 * 4
                return a

            qT = tmp([128, 3, 2048], BF16)
            qTB = {(c, n): pbuf(f"att.q{c}_{n}") for c in range(3) for n in range(4)}
            kT = tmp([128, 2, 2048], BF16)
            kTB = {n: pbuf(f"att.k{n}") for n in range(4)}
            Vd = tmp([128, 16, 2, 192], BF16)
            VdB = pbufs(16, "att.Vd")
            PtD = [tmp([128, 2, 384], BF16) for _ in range(4)]
            PtDB = pbufs(4, "att.Pt")
            rr1 = tmp([128, 384], F32)
            rr = [rr1, rr1]
            rrB = pbuf("att.rr")
            oo1 = tmp([128, 384], F32)
            oo = [oo1, oo1]
            ooB = pbuf("att.oo")
            kmB = pbuf("att.km")
            if moba:
                kmf = tmp([128, 2, 8], F32)
                kmT = tmp([128, 2, 8], BF16)
            R_start = o[0]
            hnt = []
            for _ in range(2):
                a1, a3 = tmp([128, 512], BF16), tmp([128, 512], F32)
                hnt.append((a1, None, a3, pbuf(f"att.hn{_}a"), pbuf(f"att.hn{_}b"), pbuf(f"att.hn{_}c")))
            R_end = max(o[0], R_start + 9216)
            assert R_end <= ARENA * 4, R_end
            selB = pbuf("att.sel")
            nsB = pbuf("att.ns_all")
            hnB_all = [b for t_ in hnt for b in t_[3:]]
            if not moba:
                fence(group("ssm."), group("att."))
            qi, ki = (2, 3) if moba else (0, 1)
            P.op("pool", lambda e: e.memset(Vd[:, :, :, 0:64], 1.0), writes=VdB)
            P.op("pool", lambda e: e.memset(Vd[:, :, :, 128:192], 1.0), writes=VdB)
            P.op("pool", lambda e: e.memset(kT[64:128, 0, :], 0.0), writes=[kTB[n] for n in range(4)])
            P.op("pool", lambda e: e.memset(kT[0:64, 1, :], 0.0), writes=[kTB[n] for n in range(4)])
            wi = get_w()
            pend = []
            cnt = 0

            def hn_tile(wi, col_off, n, idx, dst, dstB):
                nonlocal cnt
                bk = proj_fm(wi, col_off, n, bank_pool=(0, 1, 2))
                t = hnt[cnt % 2]
                b2 = 4 + cnt % 2
                cnt += 1
                head_norm1(bk, t)
                pend.append((bk, idx, dst, dstB, t, b2))
                if len(pend) > 1:
                    a_ = pend.pop(0)
                    head_norm2(a_[0], l, a_[1], a_[2], a_[3], a_[4], a_[5])

            def hn_flush():
                while pend:
                    a_ = pend.pop(0)
                    head_norm2(a_[0], l, a_[1], a_[2], a_[3], a_[4], a_[5])

            for n in range(4):
                hn_tile(wi, 384, n, ki, ("k", n, kT), kTB[n])
            for c in range(3):
                for n in range(4):
                    hn_tile(wi, c * 128, n, qi, qT[:, c, n * 512:(n + 1) * 512], qTB[(c, n)])
            hn_flush()
            wi = get_w()
            for i in range(NT):
                bk = 6 + i % 2
                for k in range(8):
                    P.op("pe", lambda e, k=k, bk=bk, i=i: e.matmul(PS[bk][:, 0:128], lhsT=hT[:, k, i * 128:(i + 1) * 128], rhs=Wb[wi][:, k, 384:512],
                                                                  start=(k == 0), stop=(k == 7)),
                         reads=[WbB[wi], hTB[i // 4]], writes=[PSB[bk]])
                P.op("act", lambda e, bk=bk, i=i: e.activation(out=Vd[:, i, :, 64:128], in_=PS[bk][:, 0:128].rearrange("p (a b) -> p a b", a=2), func=AF.Copy),
                     reads=[PSB[bk]], writes=[VdB[i]])
            for c in range(3):
                for n in range(4):
                    bk = proj_fm(wi, c * 128, n, bank_pool=(0, 1, 2))
                    P.op("act", lambda e, c=c, n=n, bk=bk: e.activation(out=yT[:, ych + c, n * 512:(n + 1) * 512], in_=PS[bk], func=AF.Silu),
                         reads=[PSB[bk]], writes=[yTB[(ych + c, n)]])
            prefetch_w(1)
            if stop == "attn_proj":
                return

            if moba:
                o[0] = R_start
                ns_all = tmp([128, 8, 2, 128], BF16)
                nsel = tmp([128, 8, 2, 128], BF16)
                gm = tmp([128, 8, 2, 8], F32)
                top8 = tmp([128, 8, 2, 8], F32)
                assert o[0] <= R_end
                fence(hnB_all, [selB, nsB])
                P.op("dve", lambda e: e.tensor_reduce(out=kmf, in_=kT.rearrange("p a (n t) -> p a n t", n=8), axis=AX.X, op=ALU.add),
                     reads=[kTB[n] for n in range(4)], writes=[kmB])
                P.op("dve", lambda e: e.tensor_scalar(out=kmT, in0=kmf, scalar1=1.0 / 256, scalar2=None, op0=ALU.mult), reads=[kmB], writes=[kmB])
                P.op("pool", lambda e: e.memset(nsel, 0.0), writes=[selB])
                gps = [PS[6][:, 0:64].rearrange("p (i b) -> p i b", i=8), PS[7][:, 0:64].rearrange("p (i b) -> p i b", i=8)]
                for ii in range(8):
                    i = 8 + ii
                    for kv in range(2):
                        for s_ in range(3):
                            P.op("pe", lambda e, kv=kv, i=i, ii=ii, s_=s_: e.matmul(
                                gps[kv][:, ii, :], lhsT=qT[:, s_, i * 128:(i + 1) * 128], rhs=kmT[:, kv, :], start=(s_ == 0), stop=(s_ == 2)),
                                reads=[qTB[(s_, i // 4)], kmB], writes=[PSB[6 + kv]])
                for kv in range(2):
                    P.op("dve", lambda e, kv=kv: e.tensor_tensor(out=gm[:, :, kv, :], in0=gps[kv], in1=bmask[:, 8:16, kv, :], op=ALU.add),
                         reads=[PSB[6 + kv], cB], writes=[selB])
                for ii in range(8):
                    for kv in range(2):
                        P.op("dve", lambda e, ii=ii, kv=kv: e.max(out=top8[:, ii, kv, :], in_=gm[:, ii, kv, :]), reads=[selB], writes=[selB])
                for ii in range(8):
                    for kv in range(2):
                        P.op("dve", lambda e, ii=ii, kv=kv: e.tensor_scalar(out=nsel[:, ii, kv, 0:8], in0=gm[:, ii, kv, :], scalar1=top8[:, ii, kv, 2:3], scalar2=None, op0=ALU.is_lt),
                             reads=[selB], writes=[selB])
                for bq in range(2):
                    tpn = PS[6 + bq].bitcast(BF16)
                    for i4 in range(4):
                        for kv in range(2):
                            ii = bq * 4 + i4
                            slot = i4 * 2 + kv
                            P.op("pe", lambda e, tpn=tpn, ii=ii, kv=kv, slot=slot: e.transpose(out=tpn[:, slot * 128:(slot + 1) * 128], in_=nsel[:, ii, kv, :], identity=identb),
                                 reads=[selB, cB], writes=[PSB[6 + bq]])
                    P.op("act", lambda e, tpn=tpn, bq=bq: e.activation(out=ns_all[:, bq * 4:(bq + 1) * 4], in_=tpn[:, 0:1024].rearrange("p (i a b) -> p i a b", i=4, a=2), func=AF.Copy),
                         reads=[PSB[6 + bq]], writes=[nsB])

            NP2 = 4
            pairs = []
            for i in range(NT):
                kts = list(range(0, i + 1)) if moba else ([i - 1, i] if i > 0 else [i])
                for jj, kt in enumerate(kts):
                    pairs.append(dict(i=i, kt=kt, first=(jj == 0), last=(jj == len(kts) - 1), n=len(pairs)))

            def stageA(p):
                i, kt, n_ = p["i"], p["kt"], p["n"]
                qb = i // 2
                sbig = PSbig[n_ % 2]
                bks = [PSB[2 * (n_ % 2)], PSB[2 * (n_ % 2) + 1]]
                pt, ptB = PtD[n_ % NP2], PtDB[n_ % NP2]
                sel_here = moba and qb >= 4 and (kt // 2) < qb
                mk = None
                if kt == i:
                    mk = 0
                elif not moba:
                    mk = 1
                for kv in range(2):
                    so = sbig[:, kv * 512:kv * 512 + 384].rearrange("p (a b) -> p a b", a=3)
                    first = True
                    if mk is not None:
                        P.op("pe", lambda e, so=so, first=first: e.matmul(so, lhsT=identb, rhs=amask[:, mk, :].rearrange("p (a b) -> p a b", a=3), start=first, stop=False),
                             reads=[cB], writes=[bks[kv]])
                        first = False
                    if sel_here:
                        P.op("pe", lambda e, so=so, kv=kv, first=first: e.matmul(so, lhsT=kaug[:, kt // 2, :],
                                                                                rhs=ns_all[:, i - 8, kv:kv + 1, :].to_broadcast([128, 3, 128]), start=first, stop=False),
                             reads=[nsB, cB], writes=[bks[kv]])
                        first = False
                    P.op("pe", lambda e, so=so, kv=kv, first=first: e.matmul(so, lhsT=kT[:, kv, kt * 128:(kt + 1) * 128], rhs=qT[:, :, i * 128:(i + 1) * 128],
                                                                            start=first, stop=True),
                         reads=[kTB[kt // 4]] + [qTB[(c, i // 4)] for c in range(3)], writes=[bks[kv]])
                P.op("act", lambda e: e.activation(out=pt, in_=sbig.rearrange("p (k c) -> p k c", k=2)[:, :, 0:384], func=AF.Exp, scale=0.125),
                     reads=bks, writes=[ptB])

            def stageB(p):
                i, kt, n_ = p["i"], p["kt"], p["n"]
                par = i % 2
                pt, ptB = PtD[n_ % NP2], PtDB[n_ % NP2]
                first, last = p["first"], p["last"]
                for kv in range(2):
                    nb_ = 4 + kv + 2 * par
                    vl = Vd[:, kt, 0, 64:192] if kv == 0 else Vd[:, kt, 1, 0:128]
                    P.op("pe", lambda e, nb_=nb_, vl=vl, kv=kv: e.matmul(PS[nb_][:, 0:384], lhsT=vl, rhs=pt[:, kv, :], start=first, stop=last),
                         reads=[ptB, VdB[kt]], writes=[PSB[nb_]])
                if not last:
                    return
                r_, o_ = rr1, oo1
                nbs = [4 + kv + 2 * par for kv in range(2)]
                for kv in range(2):
                    nb_ = nbs[kv]
                    H = slice(kv * 64, kv * 64 + 64)
                    Hd = slice((1 - kv) * 64, (1 - kv) * 64 + 64)
                    if moba:
                        P.op("dve", lambda e, H=H, Hd=Hd, nb_=nb_: e.reciprocal(out=r_[H], in_=PS[nb_][Hd, 0:384]), reads=[PSB[nb_]], writes=[rrB])
                    else:
                        P.op("dve", lambda e, H=H, Hd=Hd, nb_=nb_, kv=kv: e.tensor_tensor(
                            out=r_[H].rearrange("p (a b) -> p a b", a=3), in0=PS[nb_][Hd, 0:384].rearrange("p (a b) -> p a b", a=3),
                            in1=es[Hd, l, kv * 3:kv * 3 + 3].unsqueeze(2).to_broadcast([64, 3, 128]), op=ALU.add),
                            reads=[PSB[nb_], cB], writes=[rrB])
                if not moba:
                    P.op("act", lambda e: e.activation(out=r_, in_=r_, func=AF.Ln), reads=[rrB], writes=[rrB])
                    P.op("act", lambda e: e.activation(out=r_, in_=r_, func=AF.Exp, scale=-1.0), reads=[rrB], writes=[rrB])
                for kv in range(2):
                    nb_ = nbs[kv]
                    H = slice(kv * 64, kv * 64 + 64)
                    P.op("dve", lambda e, H=H, nb_=nb_: e.tensor_tensor(out=o_[H], in0=PS[nb_][H, 0:384], in1=r_[H], op=ALU.mult),
                         reads=[PSB[nb_], rrB], writes=[ooB])
                dst = yT[:, ych:ych + 3, i * 128:(i + 1) * 128]
                P.op("pool", lambda e, dst=dst: e.tensor_tensor(out=dst, in0=o_.rearrange("p (a b) -> p a b", a=3), in1=dst, op=ALU.mult),
                     reads=[ooB], writes=[yTB[(ych + c, i // 4)] for c in range(3)])

            LOOK = 3
            for idx in range(len(pairs) + LOOK):
                if idx < len(pairs):
                    stageA(pairs[idx])
                if idx >= LOOK:
                    stageB(pairs[idx - LOOK])
            nm = "ymoba" if moba else "yswa"
            tap(nm, yT[:, ych:ych + 3, :], [yTB[(c, n)] for c in range(ych, ych + 3) for n in range(4)])

        def outproj_phase(l, s, last):
            wis = [get_w(), get_w(prefetch=False)]
            for i in range(NT):
                for hfc in range(2):
                    bk = (i * 2 + hfc) % 2
                    wi = wis[hfc]
                    for k in range(8):
                        P.op("pe", lambda e, k=k, bk=bk, i=i, wi=wi: e.matmul(PS[bk], lhsT=yT[:, k, i * 128:(i + 1) * 128], rhs=Wb[wi][:, k, :],
                                                                             start=(k == 0), stop=(k == 7)),
                             reads=[WbB[wi]] + [yTB[(c, i // 4)] for c in range(8)], writes=[PSB[bk]])
                    xs = X[:, i, hfc * 512:(hfc + 1) * 512]
                    P.op("dve", lambda e, xs=xs, bk=bk: e.tensor_tensor(out=xs, in0=PS[bk], in1=xs, op=ALU.add), reads=[PSB[bk]], writes=[XB[i]])
                if last:
                    P.dma("sp", out_d[s, i * 128:(i + 1) * 128, :], X[:, i, :], reads=[XB[i]], is_output=True)
            prefetch_w()

        for s in range(nseq):
            for i in range(NT):
                P.dma("sp", X[:, i, :], x_d[s, i * 128:(i + 1) * 128, :], writes=[XB[i]])
            for l in range(2):
                norm_phase(l)
                tap(f"hT{l}", hT, hTB)
                if stop == "norm":
                    break
                ssm_phase(l)
                if stop == "ssm":
                    break
                attn_phase(l, False)
                if stop in ("swa", "attn_proj", "attn_1", "attn_A", "attn_B"):
                    break
                attn_phase(l, True)
                if stop == "moba":
                    break
                outproj_phase(l, s, l == 1)
                if stop == "layer0":
                    break
            if stop is not None:
                break
        P.finish()
        P.emit()
    return nc, tap_out


_CACHE = {}


def kernel(**inputs):
    inp = {k: np.asarray(v) for k, v in inputs.items()}
    n_cores = 8
    nseq = inp["x"].shape[0] // n_cores
    if "nc" not in _CACHE:
        _CACHE["nc"] = build(nseq)[0]
    nc = _CACHE["nc"]
    shared = dict(_host_consts())
    shared.update(_host_params(inp))
    shared["w_in"] = np.ascontiguousarray(inp["w_in"], dtype=np.float32)
    shared["w_out"] = np.ascontiguousarray(inp["w_out"], dtype=np.float32)
    shared["gluw"] = np.ascontiguousarray(inp["ssm_glu_w"], dtype=np.float32)
    x = np.ascontiguousarray(inp["x"], dtype=np.float32)
    in_maps = []
    for c in range(n_cores):
        m = dict(shared)
        m["x"] = x[c * nseq:(c + 1) * nseq]
        in_maps.append(m)
    res = run_bass_kernel_spmd(nc, in_maps, core_ids=list(range(n_cores)))
    out = np.concatenate([r["out"] for r in res.results], axis=0)
    return out.astype(np.float32)
```
